# Optimizing a Trainium2 kernel written in Bass

```python
import jax, jax.numpy as jnp
from jax import lax
import numpy as np

D_MODEL = 1024
BATCH = 4
SEQ = 4096
DEPTH = 2

CHUNK = 64
EPS = 1e-6
D_CONV = D_MODEL // 2
N_CONV_GROUPS = 4
CONV_WIDTH = 31
D_POOL = D_MODEL // 2
POOL_WINDOWS = (2, 4, 8, 16)
N_POOL_GROUPS = len(POOL_WINDOWS)
POOL_GROUP = D_POOL // N_POOL_GROUPS
HG_HEADS = 8
HG_DK = D_MODEL // HG_HEADS
HG_DV = HG_DK
D_HG = HG_HEADS * HG_DK
D_FF = -(-8 * D_MODEL // (3 * 256)) * 256
N_EVEN = (DEPTH + 1) // 2
N_ODD = DEPTH // 2

kernel_name = "hybrid_conformer_pool_hgrn2_trunk"


def rmsnorm(x, g):
    xf = x.astype(jnp.float32)
    y = xf * lax.rsqrt(jnp.mean(xf * xf, axis=-1, keepdims=True) + EPS)
    return (y * g.astype(jnp.float32)).astype(x.dtype)


def layernorm(x, g, b):
    xf = x.astype(jnp.float32)
    mu = jnp.mean(xf, axis=-1, keepdims=True)
    var = jnp.mean(jnp.square(xf - mu), axis=-1, keepdims=True)
    y = (xf - mu) * lax.rsqrt(var + EPS)
    return (y * g.astype(jnp.float32) + b.astype(jnp.float32)).astype(x.dtype)


def swiglu_ffn(h, w1, w3, w2):
    return (jax.nn.silu(h @ w1) * (h @ w3)) @ w2


def conv_pool_mixer(h, w_in, dw_w, dw_b, ln_g, ln_b, pool_w, pool_scale, w_out):
    B, T, _ = h.shape
    z = h @ w_in
    a_val = z[..., :D_CONV]
    a_gate = z[..., D_CONV:2 * D_CONV]
    u = z[..., 2 * D_CONV:]

    a = a_val * jax.nn.sigmoid(a_gate)
    a = lax.conv_general_dilated(
        a, dw_w[:, None, :].astype(a.dtype), window_strides=(1,),
        padding=[(CONV_WIDTH - 1, 0)],
        dimension_numbers=('NWC', 'WIO', 'NWC'),
        feature_group_count=D_CONV) + dw_b
    a = jax.nn.silu(layernorm(a, ln_g, ln_b))

    ug = u.reshape(B, T, N_POOL_GROUPS, POOL_GROUP)
    cs = jnp.cumsum(ug.astype(jnp.float32), axis=1)
    cs = jnp.pad(cs, ((0, 0), (1, 0), (0, 0), (0, 0)))
    t = jnp.arange(T)
    means = []
    for g, w in enumerate(POOL_WINDOWS):
        lo = jnp.maximum(t + 1 - w, 0)
        s = cs[:, 1:, g] - cs[:, lo, g]
        cnt = jnp.minimum(t + 1, w).astype(jnp.float32)[:, None]
        means.append(s / cnt)
    pooled = jnp.stack(means, axis=2)
    d = (pooled - ug.astype(jnp.float32)).astype(u.dtype)
    p = jnp.einsum('btgc,gcd->btgd', d, pool_w).reshape(B, T, D_POOL) * pool_scale

    return jnp.concatenate([a, p], axis=-1) @ w_out


def hgrn2_mixer(h, w_in, lb, gn_g, w_out):
    B, T, _ = h.shape
    NC = T // CHUNK
    z = h @ w_in
    q, f_raw, i, g = jnp.split(z, 4, axis=-1)
    f = lb + (1.0 - lb) * jax.nn.sigmoid(f_raw.astype(jnp.float32))
    log_f = jnp.log(f)
    k = 1.0 - f

    def to_chunks(a, dh):
        a = a.astype(jnp.float32).reshape(B, NC, CHUNK, HG_HEADS, dh)
        return a.transpose(1, 0, 3, 2, 4)

    qc = to_chunks(q, HG_DK)
    kc = to_chunks(k, HG_DK)
    vc = to_chunks(i, HG_DV)
    bc = jnp.cumsum(to_chunks(log_f, HG_DK), axis=3)
    tril = jnp.tril(jnp.ones((CHUNK, CHUNK), dtype=bool))

    def step(S, inp):
        qt, kt, vt, bt = inp
        diff = bt[:, :, :, None, :] - bt[:, :, None, :, :]
        decay = jnp.exp(jnp.where(tril[:, :, None], diff, -jnp.inf))
        scores = jnp.einsum('bhtk,bhsk,bhtsk->bhts', qt, kt, decay)
        o = (jnp.einsum('bhts,bhsv->bhtv', scores, vt)
             + jnp.einsum('bhtk,bhkv->bhtv', qt * jnp.exp(bt), S))
        b_last = bt[:, :, -1:, :]
        S = (jnp.exp(b_last[:, :, 0, :])[..., None] * S
             + jnp.einsum('bhsk,bhsv->bhkv', kt * jnp.exp(b_last - bt), vt))
        return S, o

    S0 = jnp.zeros((B, HG_HEADS, HG_DK, HG_DV), jnp.float32)
    _, o = lax.scan(step, S0, (qc, kc, vc, bc))
    o = o.transpose(1, 0, 3, 2, 4).reshape(B, T, HG_HEADS, HG_DV)
    o = o * lax.rsqrt(jnp.mean(o * o, axis=-1, keepdims=True) + EPS)
    o = o * gn_g.astype(jnp.float32).reshape(HG_HEADS, HG_DV)
    o = o.reshape(B, T, D_HG).astype(h.dtype) * jax.nn.silu(g)
    return o @ w_out


def setup_inputs(seed: int = 0) -> dict:
    key = jax.random.key(seed)
    ks = jax.random.split(key, 20)
    nrm = lambda k, shape, s: jax.random.normal(k, shape, jnp.float32) * s
    gain = lambda k, shape: 1.0 + 0.05 * jax.random.normal(k, shape, jnp.float32)
    d_in0 = 2 * D_CONV + D_POOL
    return {
        "x": jax.random.normal(ks[0], (BATCH, SEQ, D_MODEL), jnp.float32),
        "norm_mix_g": gain(ks[1], (DEPTH, D_MODEL)),
        "norm_ffn_g": gain(ks[2], (DEPTH, D_MODEL)),
        "final_g": gain(ks[3], (D_MODEL,)),
        "cp_w_in": nrm(ks[4], (N_EVEN, D_MODEL, d_in0), D_MODEL ** -0.5),
        "cp_dw_w": nrm(ks[5], (N_EVEN, CONV_WIDTH, D_CONV), CONV_WIDTH ** -0.5),
        "cp_dw_b": nrm(ks[6], (N_EVEN, D_CONV), 0.01),
        "cp_ln_g": gain(ks[7], (N_EVEN, D_CONV)),
        "cp_ln_b": nrm(ks[8], (N_EVEN, D_CONV), 0.01),
        "cp_pool_w": nrm(ks[9], (N_EVEN, N_POOL_GROUPS, POOL_GROUP, POOL_GROUP), POOL_GROUP ** -0.5),
        "cp_pool_scale": gain(ks[10], (N_EVEN, D_POOL)),
        "cp_w_out": nrm(ks[11], (N_EVEN, D_CONV + D_POOL, D_MODEL), (D_CONV + D_POOL) ** -0.5),
        "hg_w_in": nrm(ks[12], (N_ODD, D_MODEL, 4 * D_HG), D_MODEL ** -0.5),
        "hg_lb_logits": nrm(ks[13], (DEPTH, D_HG), 0.1),
        "hg_gn_g": gain(ks[14], (N_ODD, D_HG)),
        "hg_w_out": nrm(ks[15], (N_ODD, D_HG, D_MODEL), D_HG ** -0.5),
        "ffn_w1": nrm(ks[16], (DEPTH, D_MODEL, D_FF), D_MODEL ** -0.5),
        "ffn_w3": nrm(ks[17], (DEPTH, D_MODEL, D_FF), D_MODEL ** -0.5),
        "ffn_w2": nrm(ks[18], (DEPTH, D_FF, D_MODEL), D_FF ** -0.5),
    }


def reference(x, norm_mix_g, norm_ffn_g, final_g, cp_w_in, cp_dw_w, cp_dw_b,
              cp_ln_g, cp_ln_b, cp_pool_w, cp_pool_scale, cp_w_out, hg_w_in,
              hg_lb_logits, hg_gn_g, hg_w_out, ffn_w1, ffn_w3, ffn_w2):
    p = jax.nn.softmax(hg_lb_logits.astype(jnp.float32), axis=0)
    lower_bounds = jnp.cumsum(p, axis=0) - p[0:1]
    h = x
    for layer in range(DEPTH):
        hn = rmsnorm(h, norm_mix_g[layer])
        j = layer // 2
        if layer % 2 == 0:
            mix = conv_pool_mixer(hn, cp_w_in[j], cp_dw_w[j], cp_dw_b[j], cp_ln_g[j],
                                  cp_ln_b[j], cp_pool_w[j], cp_pool_scale[j], cp_w_out[j])
        else:
            mix = hgrn2_mixer(hn, hg_w_in[j], lower_bounds[layer], hg_gn_g[j], hg_w_out[j])
        h = h + mix
        h = h + swiglu_ffn(rmsnorm(h, norm_ffn_g[layer]), ffn_w1[layer], ffn_w3[layer], ffn_w2[layer])
    return rmsnorm(h, final_g)
```

```python
import numpy as np
import concourse.bass as bass
import concourse.mybir as mybir

F32 = mybir.dt.float32
BF16 = mybir.dt.bfloat16
ALU = mybir.AluOpType
AF = mybir.ActivationFunctionType

ENGS = ["pe", "act", "dve", "pool", "sp"]
EIDX = {e: i for i, e in enumerate(ENGS)}
GR = 64
NDMASEM = 8


def _esz(dt):
    return 2 if dt == BF16 else 4


class Space:
    def __init__(self, nbytes):
        n = (nbytes + GR - 1) // GR
        self.lw = np.full(n, -1, np.int64)
        self.rd = np.full((len(ENGS), n), -1, np.int64)
        self.dmard = {}


class Prog:
    def __init__(self, nc, sb, ps, sb_bytes, ps_bytes):
        self.nc = nc
        self.sb = sb
        self.ps = ps
        self.spaces = {"sb": Space(sb_bytes), "ps": Space(ps_bytes)}
        self.ops = []
        self.sb_off = 0
        self.sb_bytes = sb_bytes
        self.psbank = 0
        self.ndma = {e: 0 for e in ENGS}
        self.ncc = 0
        self.inames = {}
        self.stage = ""

    def alloc(self, nbytes, at=None):
        nbytes = (nbytes + GR - 1) // GR * GR
        if at is None:
            at = self.sb_off
            self.sb_off += nbytes
            assert self.sb_off <= self.sb_bytes, (self.sb_off, self.sb_bytes)
        return at

    def view(self, off, shape, dt):
        n = int(np.prod(shape))
        es = _esz(dt)
        assert off % 4 == 0
        w = (n * es + 3) // 4
        ap = self.sb[:, off // 4: off // 4 + w]
        if dt != F32:
            ap = ap.bitcast(dt)
            ap = ap[:, 0:n]
        if len(shape) == 2:
            ap = ap.rearrange("p (a b) -> p a b", a=shape[0])
        elif len(shape) == 3:
            ap = ap.rearrange("p (a b c) -> p a b c", a=shape[0], b=shape[1])
        return ap

    def buf(self, shape, dt, at=None):
        n = int(np.prod(shape)) * _esz(dt)
        off = self.alloc(n, at)
        return self.view(off, shape, dt)

    def psum(self, dt=F32, bank=None):
        if bank is None:
            bank = self.psbank
            self.psbank = (self.psbank + 1) % 8
        ap = self.ps[:, bank * 512:(bank + 1) * 512]
        if dt != F32:
            ap = ap.bitcast(dt)
        return ap

    @staticmethod
    def extent(ap):
        apl = ap.ap
        pstep = apl[0][0]
        off = int(ap.offset)
        es = _esz(ap.dtype)
        lo = off % pstep if pstep > 0 else off
        hi = lo + 1
        for st, cnt in apl[1:]:
            hi += abs(st) * (cnt - 1)
        space = "ps" if "PSUM" in str(ap.space).upper() or "PSUM" in type(ap.tensor).__name__.upper() else "sb"
        return space, lo * es, hi * es

    def op(self, eng, fn, reads=(), writes=(), dma=False, name="", extra=(), cc=False):
        oid = len(self.ops)
        ei = EIDX[eng]
        deps = set(extra)
        racc = [self.extent(a) for a in reads if a is not None and not self._isdram(a)]
        wacc = [self.extent(a) for a in writes if a is not None and not self._isdram(a)]
        for sp, lo, hi in racc:
            s = self.spaces[sp]
            g0, g1 = lo // GR, (hi + GR - 1) // GR
            for w in np.unique(s.lw[g0:g1]):
                if w >= 0:
                    deps.add(int(w))
        for sp, lo, hi in wacc:
            s = self.spaces[sp]
            g0, g1 = lo // GR, (hi + GR - 1) // GR
            for w in np.unique(s.lw[g0:g1]):
                if w >= 0:
                    deps.add(int(w))
            m = s.rd[:, g0:g1].max(axis=1)
            for v in m:
                if v >= 0:
                    deps.add(int(v))
            if s.dmard:
                for g in range(g0, g1):
                    st = s.dmard.pop(g, None)
                    if st:
                        deps.update(st)
        for sp, lo, hi in racc:
            s = self.spaces[sp]
            g0, g1 = lo // GR, (hi + GR - 1) // GR
            if dma:
                for g in range(g0, g1):
                    s.dmard.setdefault(g, set()).add(oid)
            else:
                np.maximum(s.rd[ei, g0:g1], oid, out=s.rd[ei, g0:g1])
        for sp, lo, hi in wacc:
            s = self.spaces[sp]
            g0, g1 = lo // GR, (hi + GR - 1) // GR
            s.lw[g0:g1] = oid
            s.rd[:, g0:g1] = -1
        deps.discard(oid)
        o = dict(id=oid, eng=eng, fn=fn, deps=deps, dma=dma, name=name, stage=getattr(self, "stage", ""))
        if cc:
            o["cc"] = self.ncc
            self.ncc += 1
        elif dma:
            o["dman"] = self.ndma[eng]
            self.ndma[eng] += 1
        self.ops.append(o)
        return oid

    @staticmethod
    def _isdram(ap):
        return "DRam" in type(ap.tensor).__name__

    def emit(self, final_dmas=()):
        nc = self.nc
        ops = self.ops
        fid = len(ops)
        ops.append(dict(id=fid, eng="sp", fn=None, deps=set(final_dmas), dma=False, name="final"))
        signaling = set()
        for o in ops:
            for d in o["deps"]:
                if not ops[d]["dma"]:
                    if ops[d]["eng"] == "pe" and o["eng"] == "pe":
                        continue
                    signaling.add(d)
        sigidx = {}
        cnt = {e: 0 for e in ENGS}
        for o in ops:
            if o["id"] in signaling:
                cnt[o["eng"]] += 1
                sigidx[o["id"]] = cnt[o["eng"]]
        self.sigcount = dict(cnt)
        import contextlib
        with contextlib.ExitStack() as st:
            esem = {e: st.enter_context(nc.semaphore("s_" + e)) for e in ENGS}
            dsem = {e: [st.enter_context(nc.semaphore("d_%s%d" % (e, i))) for i in range(NDMASEM)]
                    for e in ENGS if self.ndma[e] > 0}
            ccsem = [st.enter_context(nc.semaphore("cc%d" % i)) for i in range(self.ncc)]
            block = st.enter_context(nc.Block())

            def stream(ename):
                def run(e):
                    waited = {}
                    for o in ops:
                        if o["eng"] != ename:
                            continue
                        need = {}
                        for d in o["deps"]:
                            od = ops[d]
                            if "cc" in od:
                                key = ("c", "cc", od["cc"])
                                val = 1
                            elif od["dma"]:
                                n = od["dman"]
                                key = ("d", od["eng"], n % NDMASEM)
                                val = 16 * (n // NDMASEM + 1)
                            else:
                                if od["eng"] == "pe" and ename == "pe":
                                    continue
                                key = ("e", od["eng"])
                                val = sigidx[d]
                            if need.get(key, 0) < val:
                                need[key] = val
                        if o["dma"] and "cc" not in o:
                            n = o["dman"]
                            if n >= NDMASEM:
                                key = ("d", ename, n % NDMASEM)
                                val = 16 * (n // NDMASEM)
                                if need.get(key, 0) < val:
                                    need[key] = val
                        for key, val in need.items():
                            if waited.get(key, 0) >= val:
                                continue
                            waited[key] = val
                            sem = esem[key[1]] if key[0] == "e" else (ccsem[key[2]] if key[0] == "c" else dsem[key[1]][key[2]])
                            e.wait_ge(sem, val)
                        if o["fn"] is None:
                            continue
                        ins = o["fn"](e)
                        try:
                            self.inames[ins.ins.name] = o["stage"]
                        except Exception:
                            pass
                        if "cc" in o:
                            ins.then_inc(ccsem[o["cc"]])
                        elif o["dma"]:
                            ins.then_inc(dsem[ename][o["dman"] % NDMASEM], 16)
                        elif o["id"] in signaling:
                            ins.then_inc(esem[ename], 1)
                return run

            block.tensor(stream("pe"))
            block.scalar(stream("act"))
            block.vector(stream("dve"))
            block.gpsimd(stream("pool"))
            block.sync(stream("sp"))


from concourse.bass_utils import run_bass_kernel_spmd

D = 1024
SEQ = 4096
KC = 8
T = 1024
HL = 32
XPAD = 128
DFF = 2816
NJ = DFF // 128
EPS = 1e-6
NPAR = 208


class W:
    def __init__(self, P):
        self.P = P

    def mm(self, out, lhsT, rhs, start=True, stop=True, skip=False):
        self.P.op("pe", lambda e: e.matmul(out, lhsT, rhs, start=start, stop=stop, skip_group_check=skip),
                  reads=[lhsT, rhs], writes=[out])

    def tr(self, out, in_, ident):
        self.P.op("pe", lambda e: e.transpose(out, in_, ident), reads=[in_, ident], writes=[out])

    def act(self, out, in_, func, bias=None, scale=None, eng="act"):
        kw = {}
        rd = [in_]
        if bias is not None:
            kw["bias"] = bias
            if not isinstance(bias, float):
                rd.append(bias)
        if scale is not None:
            kw["scale"] = scale
            if not isinstance(scale, float):
                rd.append(scale)
        self.P.op("act", lambda e: e.activation(out, in_, func, **kw), reads=rd, writes=[out])

    def tt(self, out, a, b, op, eng="dve"):
        self.P.op(eng, lambda e: e.tensor_tensor(out, a, b, op), reads=[a, b], writes=[out])

    def ts(self, out, a, s1, s2, op0, op1=None, eng="dve"):
        rd = [a] + [s for s in (s1, s2) if s is not None and not isinstance(s, (int, float))]
        if op1 is None:
            self.P.op(eng, lambda e: e.tensor_single_scalar(out, a, s1, op0), reads=rd, writes=[out])
        else:
            self.P.op(eng, lambda e: e.tensor_scalar(out, a, s1, s2, op0, op1), reads=rd, writes=[out])

    def stt(self, out, a, s, b, op0, op1):
        rd = [a, b] + ([] if isinstance(s, (int, float)) else [s])
        self.P.op("dve", lambda e: e.scalar_tensor_tensor(out, a, s, b, op0, op1), reads=rd, writes=[out])

    def copy(self, eng, out, in_):
        if eng == "act":
            self.P.op("act", lambda e: e.copy(out, in_), reads=[in_], writes=[out])
        else:
            self.P.op(eng, lambda e: e.tensor_copy(out, in_), reads=[in_], writes=[out])

    def dma(self, eng, out, in_):
        return self.P.op(eng, lambda e: e.dma_start(out=out, in_=in_), reads=[in_], writes=[out], dma=True)


def build_program(passes=None, dbg=None, stop=None, nmb=None):
    dbg = dbg or []
    if passes is None:
        passes = [("full", i) for i in range(nmb or 4)]
    npass = len(passes)
    nfull = sum(1 for p in passes if p[0] == "full")
    last_state = max([i for i, p in enumerate(passes) if p[0] == "state"], default=-1)
    nc = bass.Bass("TRN2", target_bir_lowering=False)
    dt_in = lambda name, shape: nc.dram_tensor(name, list(shape), F32, kind="ExternalInput").ap()
    xp = dt_in("xp", [npass * (XPAD + T), D])
    sflag_d = dt_in("sflag", [128, 1])
    par_d = dt_in("par", [128, NPAR])
    inv_d = dt_in("invcnt", [128, npass * 4 * 16])
    w_in0 = dt_in("cp_w_in", [D, 1536])
    pool_w = dt_in("cp_pool_w", [4, 128, 128])
    w_out0 = dt_in("cp_w_out", [D, D])
    hg_in = dt_in("hg_w_in", [D, 4096])
    hg_out = dt_in("hg_w_out", [D, D])
    w1 = dt_in("ffn_w1", [2, D, DFF])
    w3 = dt_in("ffn_w3", [2, D, DFF])
    w2 = dt_in("ffn_w2", [2, DFF, D])
    y = nc.dram_tensor("y", [nfull * T, D], F32, kind="ExternalOutput").ap()
    dumps = {}
    finals = []

    SBB = 206 * 1024
    import contextlib
    with contextlib.ExitStack() as stack:
        sb = stack.enter_context(nc.sbuf_tensor("SB", [128, SBB // 4], F32))
        ps = stack.enter_context(nc.psum_tensor("PS", [128, 4096], F32))
        P = Prog(nc, sb, ps, SBB, 16384)
        w = W(P)

        def dump(name, ap, dt=F32):
            if name not in dbg:
                return
            t = nc.dram_tensor("dbg_" + name, list(ap.shape), dt, kind="ExternalOutput").ap()
            finals.append(w.dma("sp", t, ap))
            dumps[name] = t

        hT = P.buf([8, T], F32)
        hTh = P.buf([8, HL], F32)
        hn = P.buf([8, HL + T], BF16)
        S = P.buf([8, 128], F32)
        Sbf = P.buf([8, 128], BF16)
        par = P.buf([NPAR], F32)
        inv = P.buf([npass, 4, 16], F32)
        sflag = P.buf([1], F32)
        ident = P.buf([128], F32)
        identb = P.buf([128], BF16)
        onesb = P.buf([128], BF16)
        onesf = P.buf([128], F32)
        ones512 = P.buf([512], F32)
        maskb = P.buf([128], F32)
        lb = P.buf([8], F32)
        oml = P.buf([8], F32)
        noml = P.buf([8], F32)
        rstd = P.buf([512], F32)
        sqr = [P.buf([512], BF16) for _ in range(4)]
        tmpf = [P.buf([512], F32) for _ in range(4)]
        ring8 = [P.buf([8, 128], BF16) for _ in range(8)]
        ring22 = [P.buf([NJ, 128], BF16) for _ in range(2)]
        ring1 = [P.buf([1, 128], BF16) for _ in range(2)]
        rstate = {"r8": 0, "r22": 0, "r1": 0, "sq": 0, "tf": 0}
        stage0 = P.sb_off
        stage_bytes = SBB - stage0

        def salloc_reset():
            P.sb_off = stage0

        def wload(wv, kc):
            if kc == 8:
                slot = ring8[rstate["r8"] % len(ring8)]; rstate["r8"] += 1
            elif kc == NJ:
                slot = ring22[rstate["r22"] % len(ring22)]; rstate["r22"] += 1
            else:
                slot = ring1[rstate["r1"] % len(ring1)]; rstate["r1"] += 1
            w.dma("pool", slot, wv.rearrange("(kc p) m -> p kc m", p=128))
            return slot

        def sqtmp():
            r = sqr[rstate["sq"] % 4]; rstate["sq"] += 1
            return r

        def ftmp():
            r = tmpf[rstate["tf"] % 4]; rstate["tf"] += 1
            return r

        P.op("pool", lambda e: e.memset(onesf, 1.0), writes=[onesf])
        P.op("pool", lambda e: e.memset(ones512, 1.0), writes=[ones512])
        P.op("pool", lambda e: e.affine_select(ident, onesf, pattern=[[-1, 128]], compare_op=ALU.is_equal,
                                                 fill=0.0, base=0, channel_multiplier=1),
             reads=[onesf], writes=[ident])
        w.copy("dve", identb, ident)
        w.copy("dve", onesb, onesf)
        P.op("pool", lambda e: e.affine_select(maskb, onesf, pattern=[[1, 128]], compare_op=ALU.is_ge,
                                                 fill=0.0, base=0, channel_multiplier=-1),
             reads=[onesf], writes=[maskb])
        P.op("pool", lambda e: e.memset(maskb[0:64, 64:128], 0.0), writes=[maskb[0:64, 64:128]])
        w.dma("sp", par, par_d)
        w.dma("sp", sflag, sflag_d)
        w.dma("sp", inv.rearrange("p a b c -> p (a b c)"), inv_d)
        G_MIX = lambda l: par[:, l * 8:(l + 1) * 8]
        G_FFN = lambda l: par[:, 16 + l * 8:16 + (l + 1) * 8]
        G_FIN = par[:, 32:40]
        DWB = par[:, 40:44]; LNG = par[:, 44:48]; LNB = par[:, 48:52]; PSC = par[:, 52:56]
        GNG = par[:, 56:64]
        DWW = par[:, 80:204].rearrange("p (j k) -> p j k", k=31)
        w.tt(lb, par[:, 72:80], par[:, 64:72], ALU.subtract)
        w.act(lb, lb, AF.Sigmoid)
        w.ts(oml, lb, -1.0, 1.0, ALU.mult, ALU.add)
        w.ts(noml, oml, -1.0, None, ALU.mult)
        P.op("dve", lambda e: e.memset(S, 0.0), writes=[S])
        P.op("dve", lambda e: e.memset(Sbf, 0.0), writes=[Sbf])

        def rmsnorm(src, dst, n, gain, nch=8):
            pt = P.psum()
            for c in range(nch):
                sq = sqtmp()
                w.act(sq[:, 0:n], src[:, c, :], AF.Square)
                w.mm(pt[:, 0:n], onesb, sq[:, 0:n], start=(c == 0), stop=(c == nch - 1))
            w.act(rstd[:, 0:n], pt[:, 0:n], AF.Ln, bias=EPS, scale=1.0 / (nch * 128))
            w.act(rstd[:, 0:n], rstd[:, 0:n], AF.Exp, scale=-0.5)
            for c in range(nch):
                w.stt(dst[:, c, :], src[:, c, :], gain[:, c:c + 1], rstd[:, 0:n], ALU.mult, ALU.mult)

        def ffn(l):
            salloc_reset()
            h1 = P.buf([NJ, T], BF16)
            for sbk in range(2):
                cs = slice(sbk * 512, (sbk + 1) * 512)
                rmsnorm(hT[:, :, cs], hn[:, :, HL + sbk * 512:HL + (sbk + 1) * 512], 512, G_FFN(l))
            for j in range(NJ):
                sa = wload(w1[l][:, j * 128:(j + 1) * 128], 8)
                sb_ = wload(w3[l][:, j * 128:(j + 1) * 128], 8)
                for sbk in range(2):
                    c0 = HL + sbk * 512
                    pa = P.psum(); pb = P.psum()
                    for kc in range(8):
                        w.mm(pa, sa[:, kc, :], hn[:, kc, c0:c0 + 512], start=(kc == 0), stop=(kc == 7))
                    for kc in range(8):
                        w.mm(pb, sb_[:, kc, :], hn[:, kc, c0:c0 + 512], start=(kc == 0), stop=(kc == 7))
                    t = ftmp()
                    w.act(t, pa, AF.Silu)
                    w.tt(h1[:, j, sbk * 512:(sbk + 1) * 512], pb, t, ALU.mult)
            for mo in range(8):
                s2 = wload(w2[l][:, mo * 128:(mo + 1) * 128], NJ)
                for sbk in range(2):
                    cs = slice(sbk * 512, (sbk + 1) * 512)
                    pt = P.psum()
                    for j in range(NJ):
                        w.mm(pt, s2[:, j, :], h1[:, j, cs], start=(j == 0), stop=(j == NJ - 1))
                    w.tt(hT[:, mo, cs], hT[:, mo, cs], pt, ALU.add)

        for m, (mode, orow) in enumerate(passes):
            P.stage = "%d:load" % m
            salloc_reset()
            xs = [P.buf([D], F32, at=stage0 + 32 * 1024 + i * 4096) for i in range(2)]
            for tt in range(-1, T // 128):
                xt = xs[(tt + 1) % 2]
                r0 = m * (XPAD + T) + XPAD + tt * 128
                w.dma("sp", xt, xp[r0:r0 + 128, :])
                for half in range(2):
                    pt = P.psum()
                    pt4 = pt.rearrange("p (a b) -> p a b", a=4)
                    for c in range(4):
                        cc = half * 4 + c
                        w.tr(pt4[:, c, :], xt[:, cc * 128:(cc + 1) * 128], ident)
                    if tt < 0:
                        w.copy("act", hTh[:, half * 4:(half + 1) * 4, :], pt4[:, :, 128 - HL:128])
                    else:
                        w.copy("act", hT[:, half * 4:(half + 1) * 4, tt * 128:(tt + 1) * 128], pt4)
            dump("hT0_%d" % m, hT)

            P.stage = "%d:L0in" % m
            salloc_reset()
            a_bf = P.buf([4, HL + T], BF16)
            u32 = P.buf([4, HL + T], F32)
            cv = P.buf([4, 512], F32)
            cvb = P.buf([4, 512], BF16)
            cvq = P.buf([4, 512], BF16)
            tA = P.buf([HL + T], F32)
            tB = P.buf([HL + T], F32)
            d_bf = P.buf([4, T], BF16)
            cat = P.buf([8, T], BF16)
            diag = [P.buf([31, 128], BF16) for _ in range(2)]
            mu = P.buf([512], F32)
            msq = P.buf([512], F32)
            rs2 = P.buf([512], F32)
            t16 = P.buf([16], F32)

            rmsnorm(hTh, hn[:, :, 0:HL], HL, G_MIX(0))
            for sbk in range(2):
                cs = slice(sbk * 512, (sbk + 1) * 512)
                rmsnorm(hT[:, :, cs], hn[:, :, HL + sbk * 512:HL + (sbk + 1) * 512], 512, G_MIX(0))
            dump("hn0_%d" % m, hn, BF16)
            blocks = [(0, HL), (HL, 512), (HL + 512, 512)]
            for j in range(4):
                sg = wload(w_in0[:, 512 + j * 128:512 + (j + 1) * 128], 8)
                sv = wload(w_in0[:, j * 128:(j + 1) * 128], 8)
                for c0, n in blocks:
                    pa = P.psum(); pb = P.psum()
                    for kc in range(8):
                        w.mm(pa[:, 0:n], sg[:, kc, :], hn[:, kc, c0:c0 + n], start=(kc == 0), stop=(kc == 7))
                    for kc in range(8):
                        w.mm(pb[:, 0:n], sv[:, kc, :], hn[:, kc, c0:c0 + n], start=(kc == 0), stop=(kc == 7))
                    t = ftmp()
                    w.act(t[:, 0:n], pa[:, 0:n], AF.Sigmoid)
                    w.tt(a_bf[:, j, c0:c0 + n], pb[:, 0:n], t[:, 0:n], ALU.mult)
            for j in range(4):
                su = wload(w_in0[:, 1024 + j * 128:1024 + (j + 1) * 128], 8)
                for c0, n in blocks:
                    pa = P.psum()
                    for kc in range(8):
                        w.mm(pa[:, 0:n], su[:, kc, :], hn[:, kc, c0:c0 + n], start=(kc == 0), stop=(kc == 7))
                    w.copy("act", u32[:, j, c0:c0 + n], pa[:, 0:n])
            dump("a_%d" % m, a_bf, BF16)
            dump("u_%d" % m, u32)
            def pool_group(g):
                    src = u32[:, g, :]
                    lo = 0
                    bufs = [tA, tB]
                    bi = 0
                    for st in [1, 2, 4, 8][:g + 1]:
                        lo += st
                        dst = bufs[bi]; bi ^= 1
                        w.tt(dst[:, lo:HL + T], src[:, lo:HL + T], src[:, lo - st:HL + T - st], ALU.add)
                        src = dst
                    wd = float(2 ** (g + 1))
                    w.stt(d_bf[:, g, :], src[:, HL:HL + T], 1.0 / wd, u32[:, g, HL:HL + T], ALU.mult, ALU.subtract)
                    w.tt(t16, src[:, HL:HL + 16], inv[:, m, g, :], ALU.mult)
                    w.tt(d_bf[:, g, 0:16], t16, u32[:, g, HL:HL + 16], ALU.subtract)
            pass
            P.stage = "%d:L0conv" % m
            def build_diag(idx):
                j = idx % 4
                dg = diag[idx % 2]
                P.op("dve", lambda e: e.tensor_tensor(
                    dg, ident.unsqueeze(1).to_broadcast([128, 31, 128]),
                    DWW[:, j, :].unsqueeze(2).to_broadcast([128, 31, 128]), ALU.mult),
                    reads=[ident, DWW[:, j, :]], writes=[dg])
            build_diag(0)
            pst = {}

            def conv_stats(idx):
                sb_, j_ = idx // 4, idx % 4
                if j_ == 0:
                    pst[sb_] = (P.psum(), P.psum())
                p1, p2 = pst[sb_]
                w.mm(p1, onesb, cvb[:, j_, :], start=(j_ == 0), stop=(j_ == 3))
                w.mm(p2, onesb, cvq[:, j_, :], start=(j_ == 0), stop=(j_ == 3))
                if j_ == 3:
                    csl = slice(sb_ * 512, (sb_ + 1) * 512)
                    w.ts(mu, p1, 1.0 / 512, None, ALU.mult)
                    w.tt(msq, mu, mu, ALU.mult)
                    w.stt(rs2, p2, 1.0 / 512, msq, ALU.mult, ALU.subtract)
                    w.act(rs2, rs2, AF.Ln, bias=EPS)
                    w.act(rs2, rs2, AF.Exp, scale=-0.5)
                    for jj in range(4):
                        t = ftmp()
                        w.tt(t, cv[:, jj, :], mu, ALU.subtract)
                        w.tt(t, t, rs2, ALU.mult)
                        w.act(cat[:, jj, csl], t, AF.Silu, bias=LNB[:, jj:jj + 1], scale=LNG[:, jj:jj + 1])

            for idx in range(8):
                sbk, j = idx // 4, idx % 4
                dg = diag[idx % 2]
                pc = P.psum()
                for k in range(31):
                    o0 = HL + sbk * 512 - (30 - k)
                    w.mm(pc, dg[:, k, :], a_bf[:, j, o0:o0 + 512], start=(k == 0), stop=(k == 30))
                if idx + 1 < 8:
                    build_diag(idx + 1)
                if idx > 0:
                    conv_stats(idx - 1)
                w.act(cv[:, j, :], pc, AF.Identity, bias=DWB[:, j:j + 1])
                w.copy("dve", cvb[:, j, :], cv[:, j, :])
                w.act(cvq[:, j, :], cv[:, j, :], AF.Square)
                if sbk == 0:
                    pool_group(j)
            conv_stats(7)
            for g in range(4):
                sp_ = wload(pool_w[g], 1)
                for sbk in range(2):
                    cs = slice(sbk * 512, (sbk + 1) * 512)
                    pt = P.psum()
                    w.mm(pt, sp_[:, 0, :], d_bf[:, g, cs])
                    w.act(cat[:, 4 + g, cs], pt, AF.Identity, scale=PSC[:, g:g + 1])
            dump("cat_%d" % m, cat, BF16)
            P.stage = "%d:L0out" % m
            for mo in range(8):
                so = wload(w_out0[:, mo * 128:(mo + 1) * 128], 8)
                for sbk in range(2):
                    cs = slice(sbk * 512, (sbk + 1) * 512)
                    pt = P.psum()
                    for kc in range(8):
                        w.mm(pt, so[:, kc, :], cat[:, kc, cs], start=(kc == 0), stop=(kc == 7))
                    w.tt(hT[:, mo, cs], hT[:, mo, cs], pt, ALU.add)
            dump("hT1_%d" % m, hT)
            if stop == "mix0":
                continue
            P.stage = "%d:ffn0" % m
            ffn(0)
            dump("hT2_%d" % m, hT)
            if stop == "ffn0":
                continue

            if mode == "state":
                P.stage = "%d:L1st" % m
                salloc_reset()
                Fs = [P.buf([8, 512], F32) for _ in range(2)]
                Ls = P.buf([8, 512], F32)
                Cs_ = P.buf([8, 520], F32)
                khs = P.buf([8, 512], BF16)
                kTs = P.buf([4, D], BF16)
                its = [P.buf([4, D], BF16) for _ in range(2)]
                decs = [P.buf([8, 4], F32) for _ in range(2)]
                P.op("dve", (lambda cz: lambda e: e.memset(cz, 0.0))(Cs_[:, :, 0:8]), writes=[Cs_[:, :, 0:8]])
                for sbk in range(2):
                    cs = slice(sbk * 512, (sbk + 1) * 512)
                    rmsnorm(hT[:, :, cs], hn[:, :, HL + sbk * 512:HL + (sbk + 1) * 512], 512, G_MIX(1))

                def st_A(sbk):
                    c0 = HL + sbk * 512
                    for h in range(8):
                        sf = wload(hg_in[:, 1024 + h * 128:1024 + (h + 1) * 128], 8)
                        pf = P.psum()
                        for kc in range(8):
                            w.mm(pf, sf[:, kc, :], hn[:, kc, c0:c0 + 512], start=(kc == 0), stop=(kc == 7))
                        w.act(Fs[sbk][:, h, :], pf, AF.Sigmoid)

                def st_I(sbk):
                    c0 = HL + sbk * 512
                    for hv in range(8):
                        si = wload(hg_in[:, 2048 + hv * 128:2048 + (hv + 1) * 128], 8)
                        pt = P.psum()
                        pt4 = pt.rearrange("p (a b) -> p a b", a=4)
                        for tl in range(4):
                            for kc in range(8):
                                w.mm(pt4[:, tl, :], hn[:, kc, c0 + tl * 128:c0 + (tl + 1) * 128], si[:, kc, :],
                                     start=(kc == 0), stop=(kc == 7), skip=True)
                        w.copy("dve", its[sbk][:, :, hv * 128:(hv + 1) * 128], pt4)

                def st_B(sbk):
                    Fq = Fs[sbk]
                    for h in range(8):
                        w.act(Ls[:, h, :], Fq[:, h, :], AF.Ln, bias=lb[:, h:h + 1], scale=oml[:, h:h + 1])
                    for h in range(8):
                        w.ts(Fq[:, h, :], Fq[:, h, :], noml[:, h:h + 1], oml[:, h:h + 1], ALU.mult, ALU.add)
                    for h in range(8):
                        P.op("dve", (lambda co, li: lambda e: e.tensor_tensor_scan(
                            co, ones512, li, 0.0, ALU.mult, ALU.add))(Cs_[:, h, 8:520], Ls[:, h, :]),
                            reads=[ones512, Ls[:, h, :]], writes=[Cs_[:, h, 8:520]])
                    Cs2 = Cs_[:, :, 7:519].rearrange("p h (c l) -> p h c l", l=128)[:, :, :, 0:1]
                    Ce2 = Cs_[:, :, 8:520].rearrange("p h (c l) -> p h c l", l=128)[:, :, :, 127:128]
                    L2 = Ls.rearrange("p h (c l) -> p h c l", l=128)
                    w.tt(L2, Ce2.to_broadcast([128, 8, 4, 128]),
                         Cs_[:, :, 8:520].rearrange("p h (c l) -> p h c l", l=128), ALU.subtract)
                    w.tt(decs[sbk].unsqueeze(3), Ce2, Cs2, ALU.subtract)
                    w.act(decs[sbk], decs[sbk], AF.Exp)
                    w.act(Ls, Ls, AF.Exp)
                    w.tt(khs, Ls, Fq, ALU.mult)

                def st_C(sbk):
                    for h in range(8):
                        ptb = P.psum(BF16)
                        ptb4 = ptb[:, 0:512].rearrange("p (a b) -> p a b", a=4)
                        for tl in range(4):
                            w.tr(ptb4[:, tl, :], khs[:, h, tl * 128:(tl + 1) * 128], identb)
                        w.copy("act" if h % 2 == 0 else "dve", kTs[:, :, h * 128:(h + 1) * 128], ptb4)

                def st_S(sbk):
                    for pr in range(4):
                        pds = [P.psum(), P.psum()]
                        for h in range(8):
                            pv = pds[h // 4].rearrange("p (a b) -> p a b", a=4)
                            w.mm(pv[:, h % 4, :], kTs[:, pr, h * 128:(h + 1) * 128],
                                 its[sbk][:, pr, h * 128:(h + 1) * 128], skip=True)
                        w.tt(S, S, decs[sbk][:, :, pr:pr + 1].to_broadcast([128, 8, 128]), ALU.mult)
                        for b_ in range(2):
                            pv = pds[b_].rearrange("p (a b) -> p a b", a=4)
                            w.tt(S[:, b_ * 4:(b_ + 1) * 4, :], S[:, b_ * 4:(b_ + 1) * 4, :], pv, ALU.add)

                st_A(0); st_I(0); st_B(0); st_A(1); st_I(1); st_C(0); st_S(0); st_B(1); st_C(1); st_S(1)
                if m == last_state:
                    w.ts(S, S, sflag[:, 0:1], None, ALU.mult)
                    w.copy("act", Sbf, S)
                continue
            P.stage = "%d:L1chain" % m
            salloc_reset()
            GS = 8 if mode == "state" else 4
            NG = 8 // GS
            kTt = P.buf([4, D], BF16)
            itok = P.buf([4, D], BF16)
            if mode == "full":
                qt = P.buf([8, 512], BF16)
                kt = P.buf([8, 512], BF16)
                o32 = P.buf([8, 512], F32)
                og = P.buf([8, T], BF16)
                SG = P.buf([8, 512], BF16)
                FL = P.buf([8, 512], F32)
                Fb = FL[:, 0:4, :]
                Lb = FL[:, 4:8, :]
            else:
                Fb = P.buf([8, 512], F32)
                Lb = P.buf([8, 512], F32)
            Cb = P.buf([GS, 520], F32)
            khat = P.buf([GS, 512], BF16)
            dec = P.buf([8, 8], F32)
            scb = P.buf([8, 128], BF16)
            P.op("dve", (lambda cz: lambda e: e.memset(cz, 0.0))(Cb[:, :, 0:8]), writes=[Cb[:, :, 0:8]])
            for sbk in range(2):
                cs = slice(sbk * 512, (sbk + 1) * 512)
                rmsnorm(hT[:, :, cs], hn[:, :, HL + sbk * 512:HL + (sbk + 1) * 512], 512, G_MIX(1))
            for sbk in range(2):
                c0 = HL + sbk * 512
                cs = slice(sbk * 512, (sbk + 1) * 512)
                P.stage = "%d:L1chain" % m
                for hg in range(NG):
                    hs = list(range(hg * GS, hg * GS + GS))
                    for i, h in enumerate(hs):
                        sf = wload(hg_in[:, 1024 + h * 128:1024 + (h + 1) * 128], 8)
                        pf = P.psum()
                        for kc in range(8):
                            w.mm(pf, sf[:, kc, :], hn[:, kc, c0:c0 + 512], start=(kc == 0), stop=(kc == 7))
                        w.act(Fb[:, i, :], pf, AF.Sigmoid)
                    if hg == 0:
                        for hv in range(8):
                            si = wload(hg_in[:, 2048 + hv * 128:2048 + (hv + 1) * 128], 8)
                            pt = P.psum()
                            pt4 = pt.rearrange("p (a b) -> p a b", a=4)
                            for tl in range(4):
                                for kc in range(8):
                                    w.mm(pt4[:, tl, :], hn[:, kc, c0 + tl * 128:c0 + (tl + 1) * 128], si[:, kc, :],
                                         start=(kc == 0), stop=(kc == 7), skip=True)
                            w.copy("dve", itok[:, :, hv * 128:(hv + 1) * 128], pt4)
                    elif mode == "full":
                        for h in range(8):
                            sgw = wload(hg_in[:, 3072 + h * 128:3072 + (h + 1) * 128], 8)
                            pg = P.psum()
                            for kc in range(8):
                                w.mm(pg, sgw[:, kc, :], hn[:, kc, c0:c0 + 512], start=(kc == 0), stop=(kc == 7))
                            w.act(SG[:, h, :], pg, AF.Silu)
                    for i, h in enumerate(hs):
                        w.act(Lb[:, i, :], Fb[:, i, :], AF.Ln, bias=lb[:, h:h + 1], scale=oml[:, h:h + 1])
                    for i, h in enumerate(hs):
                        w.ts(Fb[:, i, :], Fb[:, i, :], noml[:, h:h + 1], oml[:, h:h + 1], ALU.mult, ALU.add)
                    for i, h in enumerate(hs):
                        P.op("dve", (lambda co, li: lambda e: e.tensor_tensor_scan(
                            co, ones512, li, 0.0, ALU.mult, ALU.add))(Cb[:, i, 8:520], Lb[:, i, :]),
                            reads=[ones512, Lb[:, i, :]], writes=[Cb[:, i, 8:520]])
                    Cc = Cb[:, :, 8:520].rearrange("p h (c l) -> p h c l", l=64)
                    Cs = Cb[:, :, 7:519].rearrange("p h (c l) -> p h c l", l=64)[:, :, :, 0:1]
                    Ce = Cb[:, :, 8:520].rearrange("p h (c l) -> p h c l", l=64)[:, :, :, 63:64]
                    if mode == "state":
                        Cs2 = Cb[:, :, 7:519].rearrange("p h (c l) -> p h c l", l=128)[:, :, :, 0:1]
                        Ce2 = Cb[:, :, 8:520].rearrange("p h (c l) -> p h c l", l=128)[:, :, :, 127:128]
                    Lc = Lb.rearrange("p h (c l) -> p h c l", l=64)
                    if mode == "full":
                        w.tt(Lc, Cc, Cs.to_broadcast([128, GS, 8, 64]), ALU.subtract)
                    dg = dec[:, hg * GS:hg * GS + GS, :]
                    if mode == "state":
                        dg2 = dec[:, hg * GS:hg * GS + GS, 0:4]
                        L2 = Lb.rearrange("p h (c l) -> p h c l", l=128)
                        w.tt(L2, Ce2.to_broadcast([128, GS, 4, 128]),
                             Cb[:, :, 8:520].rearrange("p h (c l) -> p h c l", l=128), ALU.subtract)
                        w.tt(dg2.unsqueeze(3), Ce2, Cs2, ALU.subtract)
                        w.act(dg2, dg2, AF.Exp)
                        w.act(Lb, Lb, AF.Exp)
                        for i, h in enumerate(hs):
                            w.tt(khat[:, i, :], Lb[:, i, :], Fb[:, i, :], ALU.mult)
                    else:
                        w.tt(dg.unsqueeze(3), Ce, Cs, ALU.subtract)
                        w.act(dg, dg, AF.Exp)
                        E2 = Cb[:, :, 8:520]
                        w.act(E2, Lb, AF.Exp, scale=-1.0)
                        w.act(Lb, Lb, AF.Exp)
                        for i, h in enumerate(hs):
                            t = ftmp()
                            w.tt(t, E2[:, i, :], Fb[:, i, :], ALU.mult)
                            w.copy("act", kt[:, h, :], t)
                            w.tt(khat[:, i, :].rearrange("p (c l) -> p c l", l=64), t.rearrange("p (c l) -> p c l", l=64),
                                 dec[:, h, :].unsqueeze(2).to_broadcast([128, 8, 64]), ALU.mult)
                    for i, h in enumerate(hs):
                        if mode == "full":
                            sq_ = wload(hg_in[:, h * 128:(h + 1) * 128], 8)
                            pq = P.psum()
                            for kc in range(8):
                                w.mm(pq, sq_[:, kc, :], hn[:, kc, c0:c0 + 512], start=(kc == 0), stop=(kc == 7))
                            w.tt(qt[:, h, :], pq, Lb[:, i, :], ALU.mult)
                    for i, h in enumerate(hs):
                        ptb = P.psum(BF16)
                        ptb4 = ptb[:, 0:512].rearrange("p (a b) -> p a b", a=4)
                        for tl in range(4):
                            w.tr(ptb4[:, tl, :], khat[:, i, tl * 128:(tl + 1) * 128], identb)
                        w.copy("act", kTt[:, :, h * 128:(h + 1) * 128], ptb4)
                if sbk == 0 and mode == "full":
                    dump("qt_%d" % m, qt, BF16); dump("kt_%d" % m, kt, BF16)
                    dump("kTt_%d" % m, kTt, BF16); dump("itok_%d" % m, itok, BF16); dump("dec_%d" % m, dec)
                P.stage = "%d:L1scan" % m
                for pr in range(4):
                    if mode == "state":
                        pds = [P.psum(), P.psum()]
                        for h in range(8):
                            pv = pds[h // 4].rearrange("p (a b) -> p a b", a=4)
                            w.mm(pv[:, h % 4, :], kTt[:, pr, h * 128:(h + 1) * 128],
                                 itok[:, pr, h * 128:(h + 1) * 128], skip=True)
                        w.tt(S, S, dec[:, :, pr:pr + 1].to_broadcast([128, 8, 128]), ALU.mult)
                        for b_ in range(2):
                            pv = pds[b_].rearrange("p (a b) -> p a b", a=4)
                            w.tt(S[:, b_ * 4:(b_ + 1) * 4, :], S[:, b_ * 4:(b_ + 1) * 4, :], pv, ALU.add)
                        continue
                    tc = slice(pr * 128, (pr + 1) * 128)
                    psc = [P.psum(), P.psum()]
                    for h in range(8):
                        pv = psc[h // 4].rearrange("p (a b) -> p a b", a=4)
                        w.mm(pv[:, h % 4, :], kt[:, h, tc], qt[:, h, tc], skip=True)
                    for b_ in range(2):
                        pv = psc[b_].rearrange("p (a b) -> p a b", a=4)
                        w.tt(scb[:, b_ * 4:(b_ + 1) * 4, :], pv, maskb.unsqueeze(1).to_broadcast([128, 4, 128]), ALU.mult)
                    pso = [P.psum(), P.psum()]
                    for half in range(2):
                        hc = slice(half * 64, (half + 1) * 64)
                        pds = [P.psum(), P.psum()]
                        for h in range(8):
                            pv = pds[h // 4].rearrange("p (a b) -> p a b", a=4)
                            w.mm(pv[:, h % 4, :], kTt[hc, pr, h * 128:(h + 1) * 128], itok[hc, pr, h * 128:(h + 1) * 128],
                                 skip=True)
                        if half == 0:
                            for h in range(8):
                                pv = pso[h // 4].rearrange("p (a b) -> p a b", a=4)
                                w.mm(pv[:, h % 4, :], itok[:, pr, h * 128:(h + 1) * 128], scb[:, h, :],
                                     start=(h % 4 == 0), stop=False, skip=True)
                        for h in range(8):
                            pv = pso[h // 4].rearrange("p (a b) -> p a b", a=4)
                            w.mm(pv[:, h % 4, hc], Sbf[:, h, :], qt[:, h, pr * 128 + half * 64:pr * 128 + (half + 1) * 64],
                                 start=False, stop=(half == 1), skip=True)
                        ci = pr * 2 + half
                        w.tt(S, S, dec[:, :, ci:ci + 1].to_broadcast([128, 8, 128]), ALU.mult)
                        for b_ in range(2):
                            pv = pds[b_].rearrange("p (a b) -> p a b", a=4)
                            w.tt(S[:, b_ * 4:(b_ + 1) * 4, :], S[:, b_ * 4:(b_ + 1) * 4, :], pv, ALU.add)
                        w.copy("act", Sbf, S)
                    for b_ in range(2):
                        pv = pso[b_].rearrange("p (a b) -> p a b", a=4)
                        w.copy("act", o32[:, b_ * 4:(b_ + 1) * 4, pr * 128:(pr + 1) * 128], pv)
                if mode == "state":
                    continue
                P.stage = "%d:L1gate" % m
                SQ = kt
                w.act(SQ, o32, AF.Square)
                for h in range(8):
                    pn = P.psum()
                    w.mm(pn, onesb, SQ[:, h, :])
                    w.act(FL[:, h, :], pn, AF.Ln, bias=EPS, scale=1.0 / 128)
                w.act(FL, FL, AF.Exp, scale=-0.5)
                w.tt(FL, FL, o32, ALU.mult)
                for h in range(8):
                    w.stt(og[:, h, cs], FL[:, h, :], GNG[:, h:h + 1], SG[:, h, :], ALU.mult, ALU.mult)
            if mode == "state":
                if m == last_state:
                    w.ts(S, S, sflag[:, 0:1], None, ALU.mult)
                    w.copy("act", Sbf, S)
                continue
            dump("og_%d" % m, og, BF16)
            P.stage = "%d:L1out" % m
            for mo in range(8):
                so = wload(hg_out[:, mo * 128:(mo + 1) * 128], 8)
                for sbk in range(2):
                    cs = slice(sbk * 512, (sbk + 1) * 512)
                    pt = P.psum()
                    for kc in range(8):
                        w.mm(pt, so[:, kc, :], og[:, kc, cs], start=(kc == 0), stop=(kc == 7))
                    w.tt(hT[:, mo, cs], hT[:, mo, cs], pt, ALU.add)
            dump("hT3_%d" % m, hT)
            if stop == "mix1":
                continue
            P.stage = "%d:ffn1" % m
            ffn(1)
            dump("hT4_%d" % m, hT)

            P.stage = "%d:final" % m
            salloc_reset()
            yT = P.buf([8, 512], F32)
            yst = [P.buf([D], F32) for _ in range(2)]
            for sbk in range(2):
                cs = slice(sbk * 512, (sbk + 1) * 512)
                rmsnorm(hT[:, :, cs], yT, 512, G_FIN)
                for tl in range(4):
                    ys = yst[tl % 2]
                    for half in range(2):
                        pt = P.psum()
                        pt4 = pt.rearrange("p (a b) -> p a b", a=4)
                        for c in range(4):
                            w.tr(pt4[:, c, :], yT[:, half * 4 + c, tl * 128:(tl + 1) * 128], ident)
                        w.copy("act" if half == 0 else "dve", ys[:, half * 512:(half + 1) * 512], pt)
                    r0 = orow * T + sbk * 512 + tl * 128
                    finals.append(w.dma("sp", y[r0:r0 + 128, :], ys))
        with nc.allow_low_precision("bf16 matmul operands, fp32 accumulate"):
            P.emit(final_dmas=finals)
    nc._inames = P.inames
    return nc, dumps


def _pack_params(inp):
    pk = lambda v, n: np.ascontiguousarray(np.asarray(v, np.float32).reshape(n, 128).T)
    par = np.zeros((128, NPAR), np.float32)
    par[:, 0:16] = pk(inp["norm_mix_g"], 16)
    par[:, 16:32] = pk(inp["norm_ffn_g"], 16)
    par[:, 32:40] = pk(inp["final_g"], 8)
    par[:, 40:44] = pk(inp["cp_dw_b"][0], 4)
    par[:, 44:48] = pk(inp["cp_ln_g"][0], 4)
    par[:, 48:52] = pk(inp["cp_ln_b"][0], 4)
    par[:, 52:56] = pk(inp["cp_pool_scale"][0], 4)
    par[:, 56:64] = pk(inp["hg_gn_g"][0], 8)
    par[:, 64:72] = pk(inp["hg_lb_logits"][0], 8)
    par[:, 72:80] = pk(inp["hg_lb_logits"][1], 8)
    dw = np.asarray(inp["cp_dw_w"][0], np.float32)
    par[:, 80:204] = np.transpose(dw.reshape(31, 4, 128), (2, 1, 0)).reshape(128, 124)
    return par


def _invcnt(first_tok_list):
    n = len(first_tok_list)
    inv = np.zeros((128, n, 4, 16), np.float32)
    for m, t0 in enumerate(first_tok_list):
        for g in range(4):
            wd = 2 ** (g + 1)
            tpos = t0 + np.arange(16)
            inv[:, m, g, :] = 1.0 / np.minimum(tpos + 1, wd)
    return inv.reshape(128, -1)


_NC_CACHE = {}
PASSES8 = [("state", None), ("state", None), ("full", 0), ("full", 1)]


def _segments(xseq, t0):
    seg = np.zeros((XPAD + T, D), np.float32)
    lo = t0 - XPAD
    if lo < 0:
        seg[-lo:] = xseq[0:t0 + T]
    else:
        seg[:] = xseq[lo:t0 + T]
    return seg


def make_in_maps(inp, plan):
    par = _pack_params(inp)
    x = np.asarray(inp["x"], np.float32)
    common = {
        "par": par,
        "cp_w_in": np.ascontiguousarray(inp["cp_w_in"][0]), "cp_pool_w": np.ascontiguousarray(inp["cp_pool_w"][0]),
        "cp_w_out": np.ascontiguousarray(inp["cp_w_out"][0]), "hg_w_in": np.ascontiguousarray(inp["hg_w_in"][0]),
        "hg_w_out": np.ascontiguousarray(inp["hg_w_out"][0]),
        "ffn_w1": np.ascontiguousarray(inp["ffn_w1"]), "ffn_w3": np.ascontiguousarray(inp["ffn_w3"]),
        "ffn_w2": np.ascontiguousarray(inp["ffn_w2"]),
    }
    maps = []
    for b, t0s, flag in plan:
        d = dict(common)
        d["xp"] = np.concatenate([_segments(x[b], t0) for t0 in t0s], axis=0)
        d["invcnt"] = _invcnt(t0s)
        d["sflag"] = np.full((128, 1), flag, np.float32)
        maps.append(d)
    return maps


def plan8():
    plan = []
    for b in range(4):
        plan.append((b, [0, T, 0, T], 0.0))
        plan.append((b, [0, T, 2 * T, 3 * T], 1.0))
    return plan


def kernel(**inputs):
    inp = {k: np.asarray(v) for k, v in inputs.items()}
    if "nc" not in _NC_CACHE:
        _NC_CACHE["nc"] = build_program(passes=PASSES8)[0]
    nc = _NC_CACHE["nc"]
    maps = make_in_maps(inp, plan8())
    res = run_bass_kernel_spmd(nc, maps, core_ids=list(range(8)))
    out = np.stack([np.asarray(r["y"], np.float32) for r in res.results], axis=0)
    return out.reshape(4, SEQ, D)
```

```python
import numpy as np
import concourse.bass as bass
import concourse.mybir as mybir

F32 = mybir.dt.float32
BF16 = mybir.dt.bfloat16
ALU = mybir.AluOpType
AF = mybir.ActivationFunctionType

ENGS = ["pe", "act", "dve", "pool", "sp"]
EIDX = {e: i for i, e in enumerate(ENGS)}
GR = 64
NDMASEM = 8


def _esz(dt):
    return 2 if dt == BF16 else 4


class Space:
    def __init__(self, nbytes):
        n = (nbytes + GR - 1) // GR
        self.lw = np.full(n, -1, np.int64)
        self.rd = np.full((len(ENGS), n), -1, np.int64)
        self.dmard = {}


class Prog:
    def __init__(self, nc, sb, ps, sb_bytes, ps_bytes):
        self.nc = nc
        self.sb = sb
        self.ps = ps
        self.spaces = {"sb": Space(sb_bytes), "ps": Space(ps_bytes)}
        self.ops = []
        self.sb_off = 0
        self.sb_bytes = sb_bytes
        self.psbank = 0
        self.ndma = {e: 0 for e in ENGS}
        self.ncc = 0
        self.inames = {}
        self.stage = ""

    def alloc(self, nbytes, at=None):
        nbytes = (nbytes + GR - 1) // GR * GR
        if at is None:
            at = self.sb_off
            self.sb_off += nbytes
            assert self.sb_off <= self.sb_bytes, (self.sb_off, self.sb_bytes)
        return at

    def view(self, off, shape, dt):
        n = int(np.prod(shape))
        es = _esz(dt)
        assert off % 4 == 0
        w = (n * es + 3) // 4
        ap = self.sb[:, off // 4: off // 4 + w]
        if dt != F32:
            ap = ap.bitcast(dt)
            ap = ap[:, 0:n]
        if len(shape) == 2:
            ap = ap.rearrange("p (a b) -> p a b", a=shape[0])
        elif len(shape) == 3:
            ap = ap.rearrange("p (a b c) -> p a b c", a=shape[0], b=shape[1])
        return ap

    def buf(self, shape, dt, at=None):
        n = int(np.prod(shape)) * _esz(dt)
        off = self.alloc(n, at)
        return self.view(off, shape, dt)

    def psum(self, dt=F32, bank=None):
        if bank is None:
            bank = self.psbank
            self.psbank = (self.psbank + 1) % 8
        ap = self.ps[:, bank * 512:(bank + 1) * 512]
        if dt != F32:
            ap = ap.bitcast(dt)
        return ap

    @staticmethod
    def extent(ap):
        apl = ap.ap
        pstep = apl[0][0]
        off = int(ap.offset)
        es = _esz(ap.dtype)
        lo = off % pstep if pstep > 0 else off
        hi = lo + 1
        for st, cnt in apl[1:]:
            hi += abs(st) * (cnt - 1)
        space = "ps" if "PSUM" in str(ap.space).upper() or "PSUM" in type(ap.tensor).__name__.upper() else "sb"
        return space, lo * es, hi * es

    def op(self, eng, fn, reads=(), writes=(), dma=False, name="", extra=(), cc=False):
        oid = len(self.ops)
        ei = EIDX[eng]
        deps = set(extra)
        racc = [self.extent(a) for a in reads if a is not None and not self._isdram(a)]
        wacc = [self.extent(a) for a in writes if a is not None and not self._isdram(a)]
        for sp, lo, hi in racc:
            s = self.spaces[sp]
            g0, g1 = lo // GR, (hi + GR - 1) // GR
            for w in np.unique(s.lw[g0:g1]):
                if w >= 0:
                    deps.add(int(w))
        for sp, lo, hi in wacc:
            s = self.spaces[sp]
            g0, g1 = lo // GR, (hi + GR - 1) // GR
            for w in np.unique(s.lw[g0:g1]):
                if w >= 0:
                    deps.add(int(w))
            m = s.rd[:, g0:g1].max(axis=1)
            for v in m:
                if v >= 0:
                    deps.add(int(v))
            if s.dmard:
                for g in range(g0, g1):
                    st = s.dmard.pop(g, None)
                    if st:
                        deps.update(st)
        for sp, lo, hi in racc:
            s = self.spaces[sp]
            g0, g1 = lo // GR, (hi + GR - 1) // GR
            if dma:
                for g in range(g0, g1):
                    s.dmard.setdefault(g, set()).add(oid)
            else:
                np.maximum(s.rd[ei, g0:g1], oid, out=s.rd[ei, g0:g1])
        for sp, lo, hi in wacc:
            s = self.spaces[sp]
            g0, g1 = lo // GR, (hi + GR - 1) // GR
            s.lw[g0:g1] = oid
            s.rd[:, g0:g1] = -1
        deps.discard(oid)
        o = dict(id=oid, eng=eng, fn=fn, deps=deps, dma=dma, name=name, stage=getattr(self, "stage", ""))
        if cc:
            o["cc"] = self.ncc
            self.ncc += 1
        elif dma:
            o["dman"] = self.ndma[eng]
            self.ndma[eng] += 1
        self.ops.append(o)
        return oid

    @staticmethod
    def _isdram(ap):
        return "DRam" in type(ap.tensor).__name__

    def emit(self, final_dmas=()):
        nc = self.nc
        ops = self.ops
        fid = len(ops)
        ops.append(dict(id=fid, eng="sp", fn=None, deps=set(final_dmas), dma=False, name="final"))
        signaling = set()
        for o in ops:
            for d in o["deps"]:
                if not ops[d]["dma"]:
                    if ops[d]["eng"] == "pe" and o["eng"] == "pe":
                        continue
                    signaling.add(d)
        sigidx = {}
        cnt = {e: 0 for e in ENGS}
        for o in ops:
            if o["id"] in signaling:
                cnt[o["eng"]] += 1
                sigidx[o["id"]] = cnt[o["eng"]]
        self.sigcount = dict(cnt)
        import contextlib
        with contextlib.ExitStack() as st:
            esem = {e: st.enter_context(nc.semaphore("s_" + e)) for e in ENGS}
            dsem = {e: [st.enter_context(nc.semaphore("d_%s%d" % (e, i))) for i in range(NDMASEM)]
                    for e in ENGS if self.ndma[e] > 0}
            ccsem = [st.enter_context(nc.semaphore("cc%d" % i)) for i in range(self.ncc)]
            block = st.enter_context(nc.Block())

            def stream(ename):
                def run(e):
                    waited = {}
                    for o in ops:
                        if o["eng"] != ename:
                            continue
                        need = {}
                        for d in o["deps"]:
                            od = ops[d]
                            if "cc" in od:
                                key = ("c", "cc", od["cc"])
                                val = 1
                            elif od["dma"]:
                                n = od["dman"]
                                key = ("d", od["eng"], n % NDMASEM)
                                val = 16 * (n // NDMASEM + 1)
                            else:
                                if od["eng"] == "pe" and ename == "pe":
                                    continue
                                key = ("e", od["eng"])
                                val = sigidx[d]
                            if need.get(key, 0) < val:
                                need[key] = val
                        if o["dma"] and "cc" not in o:
                            n = o["dman"]
                            if n >= NDMASEM:
                                key = ("d", ename, n % NDMASEM)
                                val = 16 * (n // NDMASEM)
                                if need.get(key, 0) < val:
                                    need[key] = val
                        for key, val in need.items():
                            if waited.get(key, 0) >= val:
                                continue
                            waited[key] = val
                            sem = esem[key[1]] if key[0] == "e" else (ccsem[key[2]] if key[0] == "c" else dsem[key[1]][key[2]])
                            e.wait_ge(sem, val)
                        if o["fn"] is None:
                            continue
                        ins = o["fn"](e)
                        try:
                            self.inames[ins.ins.name] = o["stage"]
                        except Exception:
                            pass
                        if "cc" in o:
                            ins.then_inc(ccsem[o["cc"]])
                        elif o["dma"]:
                            ins.then_inc(dsem[ename][o["dman"] % NDMASEM], 16)
                        elif o["id"] in signaling:
                            ins.then_inc(esem[ename], 1)
                return run

            block.tensor(stream("pe"))
            block.scalar(stream("act"))
            block.vector(stream("dve"))
            block.gpsimd(stream("pool"))
            block.sync(stream("sp"))


from concourse.bass_utils import run_bass_kernel_spmd

D = 1024
SEQ = 4096
KC = 8
T = 1024
HL = 32
XPAD = 128
DFF = 2816
NJ = DFF // 128
EPS = 1e-6
NPAR = 208


class W:
    def __init__(self, P):
        self.P = P

    def mm(self, out, lhsT, rhs, start=True, stop=True, skip=False):
        self.P.op("pe", lambda e: e.matmul(out, lhsT, rhs, start=start, stop=stop, skip_group_check=skip),
                  reads=[lhsT, rhs], writes=[out])

    def tr(self, out, in_, ident):
        self.P.op("pe", lambda e: e.transpose(out, in_, ident), reads=[in_, ident], writes=[out])

    def act(self, out, in_, func, bias=None, scale=None, eng="act"):
        kw = {}
        rd = [in_]
        if bias is not None:
            kw["bias"] = bias
            if not isinstance(bias, float):
                rd.append(bias)
        if scale is not None:
            kw["scale"] = scale
            if not isinstance(scale, float):
                rd.append(scale)
        self.P.op("act", lambda e: e.activation(out, in_, func, **kw), reads=rd, writes=[out])

    def tt(self, out, a, b, op, eng="dve"):
        self.P.op(eng, lambda e: e.tensor_tensor(out, a, b, op), reads=[a, b], writes=[out])

    def ts(self, out, a, s1, s2, op0, op1=None, eng="dve"):
        rd = [a] + [s for s in (s1, s2) if s is not None and not isinstance(s, (int, float))]
        if op1 is None:
            self.P.op(eng, lambda e: e.tensor_single_scalar(out, a, s1, op0), reads=rd, writes=[out])
        else:
            self.P.op(eng, lambda e: e.tensor_scalar(out, a, s1, s2, op0, op1), reads=rd, writes=[out])

    def stt(self, out, a, s, b, op0, op1):
        rd = [a, b] + ([] if isinstance(s, (int, float)) else [s])
        self.P.op("dve", lambda e: e.scalar_tensor_tensor(out, a, s, b, op0, op1), reads=rd, writes=[out])

    def copy(self, eng, out, in_):
        if eng == "act":
            self.P.op("act", lambda e: e.copy(out, in_), reads=[in_], writes=[out])
        else:
            self.P.op(eng, lambda e: e.tensor_copy(out, in_), reads=[in_], writes=[out])

    def dma(self, eng, out, in_):
        return self.P.op(eng, lambda e: e.dma_start(out=out, in_=in_), reads=[in_], writes=[out], dma=True)


def build_program(passes=None, dbg=None, stop=None, nmb=None):
    dbg = dbg or []
    if passes is None:
        passes = [("full", i) for i in range(nmb or 4)]
    npass = len(passes)
    nfull = sum(1 for p in passes if p[0] == "full")
    last_state = max([i for i, p in enumerate(passes) if p[0] == "state"], default=-1)
    nc = bass.Bass("TRN2", target_bir_lowering=False)
    dt_in = lambda name, shape: nc.dram_tensor(name, list(shape), F32, kind="ExternalInput").ap()
    xp = dt_in("xp", [npass * (XPAD + T), D])
    sflag_d = dt_in("sflag", [128, 1])
    par_d = dt_in("par", [128, NPAR])
    inv_d = dt_in("invcnt", [128, npass * 4 * 16])
    w_in0 = dt_in("cp_w_in", [D, 1536])
    pool_w = dt_in("cp_pool_w", [4, 128, 128])
    w_out0 = dt_in("cp_w_out", [D, D])
    hg_in = dt_in("hg_w_in", [D, 4096])
    hg_out = dt_in("hg_w_out", [D, D])
    w1 = dt_in("ffn_w1", [2, D, DFF])
    w3 = dt_in("ffn_w3", [2, D, DFF])
    w2 = dt_in("ffn_w2", [2, DFF, D])
    y = nc.dram_tensor("y", [nfull * T, D], F32, kind="ExternalOutput").ap()
    dumps = {}
    finals = []

    SBB = 206 * 1024
    import contextlib
    with contextlib.ExitStack() as stack:
        sb = stack.enter_context(nc.sbuf_tensor("SB", [128, SBB // 4], F32))
        ps = stack.enter_context(nc.psum_tensor("PS", [128, 4096], F32))
        P = Prog(nc, sb, ps, SBB, 16384)
        w = W(P)

        def dump(name, ap, dt=F32):
            if name not in dbg:
                return
            t = nc.dram_tensor("dbg_" + name, list(ap.shape), dt, kind="ExternalOutput").ap()
            finals.append(w.dma("sp", t, ap))
            dumps[name] = t

        hT = P.buf([8, T], F32)
        hTh = P.buf([8, HL], F32)
        hn = P.buf([8, HL + T], BF16)
        S = P.buf([8, 128], F32)
        Sbf = P.buf([8, 128], BF16)
        par = P.buf([NPAR], F32)
        inv = P.buf([npass, 4, 16], F32)
        sflag = P.buf([1], F32)
        ident = P.buf([128], F32)
        identb = P.buf([128], BF16)
        onesb = P.buf([128], BF16)
        onesf = P.buf([128], F32)
        ones512 = P.buf([512], F32)
        maskb = P.buf([128], F32)
        lb = P.buf([8], F32)
        oml = P.buf([8], F32)
        noml = P.buf([8], F32)
        rstd = P.buf([512], F32)
        sqr = [P.buf([512], BF16) for _ in range(4)]
        tmpf = [P.buf([512], F32) for _ in range(4)]
        ring8 = [P.buf([8, 128], BF16) for _ in range(8)]
        ring22 = [P.buf([NJ, 128], BF16) for _ in range(2)]
        ring1 = [P.buf([1, 128], BF16) for _ in range(2)]
        rstate = {"r8": 0, "r22": 0, "r1": 0, "sq": 0, "tf": 0}
        stage0 = P.sb_off
        stage_bytes = SBB - stage0

        def salloc_reset():
            P.sb_off = stage0

        def wload(wv, kc):
            if kc == 8:
                slot = ring8[rstate["r8"] % len(ring8)]; rstate["r8"] += 1
            elif kc == NJ:
                slot = ring22[rstate["r22"] % len(ring22)]; rstate["r22"] += 1
            else:
                slot = ring1[rstate["r1"] % len(ring1)]; rstate["r1"] += 1
            w.dma("pool", slot, wv.rearrange("(kc p) m -> p kc m", p=128))
            return slot

        def sqtmp():
            r = sqr[rstate["sq"] % 4]; rstate["sq"] += 1
            return r

        def ftmp():
            r = tmpf[rstate["tf"] % 4]; rstate["tf"] += 1
            return r

        P.op("pool", lambda e: e.memset(onesf, 1.0), writes=[onesf])
        P.op("pool", lambda e: e.memset(ones512, 1.0), writes=[ones512])
        P.op("pool", lambda e: e.affine_select(ident, onesf, pattern=[[-1, 128]], compare_op=ALU.is_equal,
                                                 fill=0.0, base=0, channel_multiplier=1),
             reads=[onesf], writes=[ident])
        w.copy("dve", identb, ident)
        w.copy("dve", onesb, onesf)
        P.op("pool", lambda e: e.affine_select(maskb, onesf, pattern=[[1, 128]], compare_op=ALU.is_ge,
                                                 fill=0.0, base=0, channel_multiplier=-1),
             reads=[onesf], writes=[maskb])
        P.op("pool", lambda e: e.memset(maskb[0:64, 64:128], 0.0), writes=[maskb[0:64, 64:128]])
        w.dma("sp", par, par_d)
        w.dma("sp", sflag, sflag_d)
        w.dma("sp", inv.rearrange("p a b c -> p (a b c)"), inv_d)
        G_MIX = lambda l: par[:, l * 8:(l + 1) * 8]
        G_FFN = lambda l: par[:, 16 + l * 8:16 + (l + 1) * 8]
        G_FIN = par[:, 32:40]
        DWB = par[:, 40:44]; LNG = par[:, 44:48]; LNB = par[:, 48:52]; PSC = par[:, 52:56]
        GNG = par[:, 56:64]
        DWW = par[:, 80:204].rearrange("p (j k) -> p j k", k=31)
        w.tt(lb, par[:, 72:80], par[:, 64:72], ALU.subtract)
        w.act(lb, lb, AF.Sigmoid)
        w.ts(oml, lb, -1.0, 1.0, ALU.mult, ALU.add)
        w.ts(noml, oml, -1.0, None, ALU.mult)
        P.op("dve", lambda e: e.memset(S, 0.0), writes=[S])
        P.op("dve", lambda e: e.memset(Sbf, 0.0), writes=[Sbf])

        def rmsnorm(src, dst, n, gain, nch=8):
            pt = P.psum()
            for c in range(nch):
                sq = sqtmp()
                if c % 2 == 0:
                    w.act(sq[:, 0:n], src[:, c, :], AF.Square)
                else:
                    w.tt(sq[:, 0:n], src[:, c, :], src[:, c, :], ALU.mult)
                w.mm(pt[:, 0:n], onesb, sq[:, 0:n], start=(c == 0), stop=(c == nch - 1))
            w.act(rstd[:, 0:n], pt[:, 0:n], AF.Ln, bias=EPS, scale=1.0 / (nch * 128))
            w.act(rstd[:, 0:n], rstd[:, 0:n], AF.Exp, scale=-0.5)
            for c in range(nch):
                w.stt(dst[:, c, :], src[:, c, :], gain[:, c:c + 1], rstd[:, 0:n], ALU.mult, ALU.mult)

        def ffn(l):
            salloc_reset()
            h1 = P.buf([NJ, T], BF16)
            for sbk in range(2):
                cs = slice(sbk * 512, (sbk + 1) * 512)
                rmsnorm(hT[:, :, cs], hn[:, :, HL + sbk * 512:HL + (sbk + 1) * 512], 512, G_FFN(l))
            for j in range(NJ):
                sa = wload(w1[l][:, j * 128:(j + 1) * 128], 8)
                sb_ = wload(w3[l][:, j * 128:(j + 1) * 128], 8)
                for sbk in range(2):
                    c0 = HL + sbk * 512
                    pa = P.psum(); pb = P.psum()
                    for kc in range(8):
                        w.mm(pa, sa[:, kc, :], hn[:, kc, c0:c0 + 512], start=(kc == 0), stop=(kc == 7))
                    for kc in range(8):
                        w.mm(pb, sb_[:, kc, :], hn[:, kc, c0:c0 + 512], start=(kc == 0), stop=(kc == 7))
                    t = ftmp()
                    w.act(t, pa, AF.Silu)
                    w.tt(h1[:, j, sbk * 512:(sbk + 1) * 512], pb, t, ALU.mult)
            for mo in range(8):
                s2 = wload(w2[l][:, mo * 128:(mo + 1) * 128], NJ)
                for sbk in range(2):
                    cs = slice(sbk * 512, (sbk + 1) * 512)
                    pt = P.psum()
                    for j in range(NJ):
                        w.mm(pt, s2[:, j, :], h1[:, j, cs], start=(j == 0), stop=(j == NJ - 1))
                    w.tt(hT[:, mo, cs], hT[:, mo, cs], pt, ALU.add)

        for m, (mode, orow) in enumerate(passes):
            P.stage = "%d:load" % m
            salloc_reset()
            xs = [P.buf([D], F32, at=stage0 + 32 * 1024 + i * 4096) for i in range(2)]
            for tt in range(-1, T // 128):
                xt = xs[(tt + 1) % 2]
                r0 = m * (XPAD + T) + XPAD + tt * 128
                w.dma("sp", xt, xp[r0:r0 + 128, :])
                for half in range(2):
                    pt = P.psum()
                    pt4 = pt.rearrange("p (a b) -> p a b", a=4)
                    for c in range(4):
                        cc = half * 4 + c
                        w.tr(pt4[:, c, :], xt[:, cc * 128:(cc + 1) * 128], ident)
                    if tt < 0:
                        w.copy("act", hTh[:, half * 4:(half + 1) * 4, :], pt4[:, :, 128 - HL:128])
                    else:
                        w.copy("act", hT[:, half * 4:(half + 1) * 4, tt * 128:(tt + 1) * 128], pt4)
            dump("hT0_%d" % m, hT)

            P.stage = "%d:L0in" % m
            salloc_reset()
            a_bf = P.buf([4, HL + T], BF16)
            u32 = P.buf([4, HL + T], F32)
            cv = P.buf([4, 512], F32)
            cvb = P.buf([4, 512], BF16)
            cvq = P.buf([4, 512], BF16)
            tA = P.buf([HL + T], F32)
            tB = P.buf([HL + T], F32)
            d_bf = P.buf([4, T], BF16)
            cat = P.buf([8, T], BF16)
            diag = [P.buf([31, 128], BF16) for _ in range(2)]
            mu = P.buf([512], F32)
            msq = P.buf([512], F32)
            rs2 = P.buf([512], F32)
            t16 = P.buf([16], F32)

            rmsnorm(hTh, hn[:, :, 0:HL], HL, G_MIX(0))
            for sbk in range(2):
                cs = slice(sbk * 512, (sbk + 1) * 512)
                rmsnorm(hT[:, :, cs], hn[:, :, HL + sbk * 512:HL + (sbk + 1) * 512], 512, G_MIX(0))
            dump("hn0_%d" % m, hn, BF16)
            blocks = [(0, HL), (HL, 512), (HL + 512, 512)]
            for j in range(4):
                sg = wload(w_in0[:, 512 + j * 128:512 + (j + 1) * 128], 8)
                sv = wload(w_in0[:, j * 128:(j + 1) * 128], 8)
                for c0, n in blocks:
                    pa = P.psum(); pb = P.psum()
                    for kc in range(8):
                        w.mm(pa[:, 0:n], sg[:, kc, :], hn[:, kc, c0:c0 + n], start=(kc == 0), stop=(kc == 7))
                    for kc in range(8):
                        w.mm(pb[:, 0:n], sv[:, kc, :], hn[:, kc, c0:c0 + n], start=(kc == 0), stop=(kc == 7))
                    t = ftmp()
                    w.act(t[:, 0:n], pa[:, 0:n], AF.Sigmoid)
                    w.tt(a_bf[:, j, c0:c0 + n], pb[:, 0:n], t[:, 0:n], ALU.mult)
            for j in range(4):
                su = wload(w_in0[:, 1024 + j * 128:1024 + (j + 1) * 128], 8)
                for c0, n in blocks:
                    pa = P.psum()
                    for kc in range(8):
                        w.mm(pa[:, 0:n], su[:, kc, :], hn[:, kc, c0:c0 + n], start=(kc == 0), stop=(kc == 7))
                    w.copy("act", u32[:, j, c0:c0 + n], pa[:, 0:n])
            dump("a_%d" % m, a_bf, BF16)
            dump("u_%d" % m, u32)
            def pool_group(g):
                    src = u32[:, g, :]
                    lo = 0
                    bufs = [tA, tB]
                    bi = 0
                    for st in [1, 2, 4, 8][:g + 1]:
                        lo += st
                        dst = bufs[bi]; bi ^= 1
                        w.tt(dst[:, lo:HL + T], src[:, lo:HL + T], src[:, lo - st:HL + T - st], ALU.add)
                        src = dst
                    wd = float(2 ** (g + 1))
                    w.stt(d_bf[:, g, :], src[:, HL:HL + T], 1.0 / wd, u32[:, g, HL:HL + T], ALU.mult, ALU.subtract)
                    w.tt(t16, src[:, HL:HL + 16], inv[:, m, g, :], ALU.mult)
                    w.tt(d_bf[:, g, 0:16], t16, u32[:, g, HL:HL + 16], ALU.subtract)
            pass
            P.stage = "%d:L0conv" % m
            def build_diag(idx):
                j = idx % 4
                dg = diag[idx % 2]
                P.op("dve", lambda e: e.tensor_tensor(
                    dg, ident.unsqueeze(1).to_broadcast([128, 31, 128]),
                    DWW[:, j, :].unsqueeze(2).to_broadcast([128, 31, 128]), ALU.mult),
                    reads=[ident, DWW[:, j, :]], writes=[dg])
            build_diag(0)
            pst = {}

            def conv_stats(idx):
                sb_, j_ = idx // 4, idx % 4
                if j_ == 0:
                    pst[sb_] = (P.psum(), P.psum())
                p1, p2 = pst[sb_]
                w.mm(p1, onesb, cvb[:, j_, :], start=(j_ == 0), stop=(j_ == 3))
                w.mm(p2, onesb, cvq[:, j_, :], start=(j_ == 0), stop=(j_ == 3))
                if j_ == 3:
                    csl = slice(sb_ * 512, (sb_ + 1) * 512)
                    w.ts(mu, p1, 1.0 / 512, None, ALU.mult)
                    w.tt(msq, mu, mu, ALU.mult)
                    w.stt(rs2, p2, 1.0 / 512, msq, ALU.mult, ALU.subtract)
                    w.act(rs2, rs2, AF.Ln, bias=EPS)
                    w.act(rs2, rs2, AF.Exp, scale=-0.5)
                    for jj in range(4):
                        t = ftmp()
                        w.tt(t, cv[:, jj, :], mu, ALU.subtract)
                        w.tt(t, t, rs2, ALU.mult)
                        w.act(cat[:, jj, csl], t, AF.Silu, bias=LNB[:, jj:jj + 1], scale=LNG[:, jj:jj + 1])

            for idx in range(8):
                sbk, j = idx // 4, idx % 4
                dg = diag[idx % 2]
                pc = P.psum()
                for k in range(31):
                    o0 = HL + sbk * 512 - (30 - k)
                    w.mm(pc, dg[:, k, :], a_bf[:, j, o0:o0 + 512], start=(k == 0), stop=(k == 30))
                if idx + 1 < 8:
                    build_diag(idx + 1)
                if idx > 0:
                    conv_stats(idx - 1)
                w.act(cv[:, j, :], pc, AF.Identity, bias=DWB[:, j:j + 1])
                w.copy("dve", cvb[:, j, :], cv[:, j, :])
                w.act(cvq[:, j, :], cv[:, j, :], AF.Square)
                if sbk == 0:
                    pool_group(j)
            conv_stats(7)
            for g in range(4):
                sp_ = wload(pool_w[g], 1)
                for sbk in range(2):
                    cs = slice(sbk * 512, (sbk + 1) * 512)
                    pt = P.psum()
                    w.mm(pt, sp_[:, 0, :], d_bf[:, g, cs])
                    w.act(cat[:, 4 + g, cs], pt, AF.Identity, scale=PSC[:, g:g + 1])
            dump("cat_%d" % m, cat, BF16)
            P.stage = "%d:L0out" % m
            for mo in range(8):
                so = wload(w_out0[:, mo * 128:(mo + 1) * 128], 8)
                for sbk in range(2):
                    cs = slice(sbk * 512, (sbk + 1) * 512)
                    pt = P.psum()
                    for kc in range(8):
                        w.mm(pt, so[:, kc, :], cat[:, kc, cs], start=(kc == 0), stop=(kc == 7))
                    w.tt(hT[:, mo, cs], hT[:, mo, cs], pt, ALU.add)
            dump("hT1_%d" % m, hT)
            if stop == "mix0":
                continue
            P.stage = "%d:ffn0" % m
            ffn(0)
            dump("hT2_%d" % m, hT)
            if stop == "ffn0":
                continue

            if mode == "state":
                P.stage = "%d:L1st" % m
                salloc_reset()
                Fs = [P.buf([8, 512], F32) for _ in range(2)]
                Ls = P.buf([8, 512], F32)
                Cs_ = P.buf([8, 520], F32)
                khs = P.buf([8, 512], BF16)
                kTs = P.buf([4, D], BF16)
                its = [P.buf([4, D], BF16) for _ in range(2)]
                decs = [P.buf([8, 4], F32) for _ in range(2)]
                P.op("dve", (lambda cz: lambda e: e.memset(cz, 0.0))(Cs_[:, :, 0:8]), writes=[Cs_[:, :, 0:8]])
                for sbk in range(2):
                    cs = slice(sbk * 512, (sbk + 1) * 512)
                    rmsnorm(hT[:, :, cs], hn[:, :, HL + sbk * 512:HL + (sbk + 1) * 512], 512, G_MIX(1))

                def st_A(sbk):
                    c0 = HL + sbk * 512
                    for h in range(8):
                        sf = wload(hg_in[:, 1024 + h * 128:1024 + (h + 1) * 128], 8)
                        pf = P.psum()
                        for kc in range(8):
                            w.mm(pf, sf[:, kc, :], hn[:, kc, c0:c0 + 512], start=(kc == 0), stop=(kc == 7))
                        w.act(Fs[sbk][:, h, :], pf, AF.Sigmoid)

                def st_I(sbk):
                    c0 = HL + sbk * 512
                    for hv in range(8):
                        si = wload(hg_in[:, 2048 + hv * 128:2048 + (hv + 1) * 128], 8)
                        pt = P.psum()
                        pt4 = pt.rearrange("p (a b) -> p a b", a=4)
                        for tl in range(4):
                            for kc in range(8):
                                w.mm(pt4[:, tl, :], hn[:, kc, c0 + tl * 128:c0 + (tl + 1) * 128], si[:, kc, :],
                                     start=(kc == 0), stop=(kc == 7), skip=True)
                        w.copy("dve", its[sbk][:, :, hv * 128:(hv + 1) * 128], pt4)

                def st_B(sbk):
                    Fq = Fs[sbk]
                    for h in range(8):
                        w.act(Ls[:, h, :], Fq[:, h, :], AF.Ln, bias=lb[:, h:h + 1], scale=oml[:, h:h + 1])
                    for h in range(8):
                        w.ts(Fq[:, h, :], Fq[:, h, :], noml[:, h:h + 1], oml[:, h:h + 1], ALU.mult, ALU.add)
                    for h in range(8):
                        P.op("dve", (lambda co, li: lambda e: e.tensor_tensor_scan(
                            co, ones512, li, 0.0, ALU.mult, ALU.add))(Cs_[:, h, 8:520], Ls[:, h, :]),
                            reads=[ones512, Ls[:, h, :]], writes=[Cs_[:, h, 8:520]])
                    Cs2 = Cs_[:, :, 7:519].rearrange("p h (c l) -> p h c l", l=128)[:, :, :, 0:1]
                    Ce2 = Cs_[:, :, 8:520].rearrange("p h (c l) -> p h c l", l=128)[:, :, :, 127:128]
                    L2 = Ls.rearrange("p h (c l) -> p h c l", l=128)
                    w.tt(L2, Ce2.to_broadcast([128, 8, 4, 128]),
                         Cs_[:, :, 8:520].rearrange("p h (c l) -> p h c l", l=128), ALU.subtract)
                    w.tt(decs[sbk].unsqueeze(3), Ce2, Cs2, ALU.subtract)
                    w.act(decs[sbk], decs[sbk], AF.Exp)
                    w.act(Ls, Ls, AF.Exp)
                    w.tt(khs, Ls, Fq, ALU.mult)

                def st_C(sbk):
                    for h in range(8):
                        ptb = P.psum(BF16)
                        ptb4 = ptb[:, 0:512].rearrange("p (a b) -> p a b", a=4)
                        for tl in range(4):
                            w.tr(ptb4[:, tl, :], khs[:, h, tl * 128:(tl + 1) * 128], identb)
                        w.copy("act" if h % 2 == 0 else "dve", kTs[:, :, h * 128:(h + 1) * 128], ptb4)

                def st_S(sbk):
                    for pr in range(4):
                        pds = [P.psum(), P.psum()]
                        for h in range(8):
                            pv = pds[h // 4].rearrange("p (a b) -> p a b", a=4)
                            w.mm(pv[:, h % 4, :], kTs[:, pr, h * 128:(h + 1) * 128],
                                 its[sbk][:, pr, h * 128:(h + 1) * 128], skip=True)
                        w.tt(S, S, decs[sbk][:, :, pr:pr + 1].to_broadcast([128, 8, 128]), ALU.mult)
                        for b_ in range(2):
                            pv = pds[b_].rearrange("p (a b) -> p a b", a=4)
                            w.tt(S[:, b_ * 4:(b_ + 1) * 4, :], S[:, b_ * 4:(b_ + 1) * 4, :], pv, ALU.add)

                st_A(0); st_I(0); st_B(0); st_A(1); st_I(1); st_C(0); st_S(0); st_B(1); st_C(1); st_S(1)
                if m == last_state:
                    w.ts(S, S, sflag[:, 0:1], None, ALU.mult)
                    w.copy("act", Sbf, S)
                continue
            P.stage = "%d:L1chain" % m
            salloc_reset()
            GS = 8 if mode == "state" else 4
            NG = 8 // GS
            kTt = P.buf([4, D], BF16)
            itok = P.buf([4, D], BF16)
            if mode == "full":
                qt = P.buf([8, 512], BF16)
                kt = P.buf([8, 512], BF16)
                o32 = P.buf([8, 512], F32)
                og = P.buf([8, T], BF16)
                SG = P.buf([8, 512], BF16)
                FL = P.buf([8, 512], F32)
                Fb = FL[:, 0:4, :]
                Lb = FL[:, 4:8, :]
            else:
                Fb = P.buf([8, 512], F32)
                Lb = P.buf([8, 512], F32)
            Cb = P.buf([GS, 520], F32)
            khat = P.buf([GS, 512], BF16)
            dec = P.buf([8, 8], F32)
            scb = P.buf([8, 128], BF16)
            P.op("dve", (lambda cz: lambda e: e.memset(cz, 0.0))(Cb[:, :, 0:8]), writes=[Cb[:, :, 0:8]])
            for sbk in range(2):
                cs = slice(sbk * 512, (sbk + 1) * 512)
                rmsnorm(hT[:, :, cs], hn[:, :, HL + sbk * 512:HL + (sbk + 1) * 512], 512, G_MIX(1))
            for sbk in range(2):
                c0 = HL + sbk * 512
                cs = slice(sbk * 512, (sbk + 1) * 512)
                P.stage = "%d:L1chain" % m
                for hg in range(NG):
                    hs = list(range(hg * GS, hg * GS + GS))
                    for i, h in enumerate(hs):
                        sf = wload(hg_in[:, 1024 + h * 128:1024 + (h + 1) * 128], 8)
                        pf = P.psum()
                        for kc in range(8):
                            w.mm(pf, sf[:, kc, :], hn[:, kc, c0:c0 + 512], start=(kc == 0), stop=(kc == 7))
                        w.act(Fb[:, i, :], pf, AF.Sigmoid)
                    if hg == 0:
                        for hv in range(8):
                            si = wload(hg_in[:, 2048 + hv * 128:2048 + (hv + 1) * 128], 8)
                            pt = P.psum()
                            pt4 = pt.rearrange("p (a b) -> p a b", a=4)
                            for tl in range(4):
                                for kc in range(8):
                                    w.mm(pt4[:, tl, :], hn[:, kc, c0 + tl * 128:c0 + (tl + 1) * 128], si[:, kc, :],
                                         start=(kc == 0), stop=(kc == 7), skip=True)
                            w.copy("act", itok[:, :, hv * 128:(hv + 1) * 128], pt4)
                    elif mode == "full":
                        for h in range(8):
                            sgw = wload(hg_in[:, 3072 + h * 128:3072 + (h + 1) * 128], 8)
                            pg = P.psum()
                            for kc in range(8):
                                w.mm(pg, sgw[:, kc, :], hn[:, kc, c0:c0 + 512], start=(kc == 0), stop=(kc == 7))
                            w.act(SG[:, h, :], pg, AF.Silu)
                    for i, h in enumerate(hs):
                        w.act(Lb[:, i, :], Fb[:, i, :], AF.Ln, bias=lb[:, h:h + 1], scale=oml[:, h:h + 1])
                    for i, h in enumerate(hs):
                        w.ts(Fb[:, i, :], Fb[:, i, :], noml[:, h:h + 1], oml[:, h:h + 1], ALU.mult, ALU.add)
                    for i, h in enumerate(hs):
                        P.op("dve", (lambda co, li: lambda e: e.tensor_tensor_scan(
                            co, ones512, li, 0.0, ALU.mult, ALU.add))(Cb[:, i, 8:520], Lb[:, i, :]),
                            reads=[ones512, Lb[:, i, :]], writes=[Cb[:, i, 8:520]])
                    Cc = Cb[:, :, 8:520].rearrange("p h (c l) -> p h c l", l=64)
                    Cs = Cb[:, :, 7:519].rearrange("p h (c l) -> p h c l", l=64)[:, :, :, 0:1]
                    Ce = Cb[:, :, 8:520].rearrange("p h (c l) -> p h c l", l=64)[:, :, :, 63:64]
                    if mode == "state":
                        Cs2 = Cb[:, :, 7:519].rearrange("p h (c l) -> p h c l", l=128)[:, :, :, 0:1]
                        Ce2 = Cb[:, :, 8:520].rearrange("p h (c l) -> p h c l", l=128)[:, :, :, 127:128]
                    Lc = Lb.rearrange("p h (c l) -> p h c l", l=64)
                    if mode == "full":
                        w.tt(Lc, Cc, Cs.to_broadcast([128, GS, 8, 64]), ALU.subtract)
                    dg = dec[:, hg * GS:hg * GS + GS, :]
                    if mode == "state":
                        dg2 = dec[:, hg * GS:hg * GS + GS, 0:4]
                        L2 = Lb.rearrange("p h (c l) -> p h c l", l=128)
                        w.tt(L2, Ce2.to_broadcast([128, GS, 4, 128]),
                             Cb[:, :, 8:520].rearrange("p h (c l) -> p h c l", l=128), ALU.subtract)
                        w.tt(dg2.unsqueeze(3), Ce2, Cs2, ALU.subtract)
                        w.act(dg2, dg2, AF.Exp)
                        w.act(Lb, Lb, AF.Exp)
                        for i, h in enumerate(hs):
                            w.tt(khat[:, i, :], Lb[:, i, :], Fb[:, i, :], ALU.mult)
                    else:
                        w.tt(dg.unsqueeze(3), Ce, Cs, ALU.subtract)
                        w.act(dg, dg, AF.Exp)
                        E2 = Cb[:, :, 8:520]
                        w.act(E2, Lb, AF.Exp, scale=-1.0)
                        w.act(Lb, Lb, AF.Exp)
                        for i, h in enumerate(hs):
                            t = ftmp()
                            w.tt(t, E2[:, i, :], Fb[:, i, :], ALU.mult)
                            w.copy("act", kt[:, h, :], t)
                            w.tt(khat[:, i, :].rearrange("p (c l) -> p c l", l=64), t.rearrange("p (c l) -> p c l", l=64),
                                 dec[:, h, :].unsqueeze(2).to_broadcast([128, 8, 64]), ALU.mult)
                    for i, h in enumerate(hs):
                        if mode == "full":
                            sq_ = wload(hg_in[:, h * 128:(h + 1) * 128], 8)
                            pq = P.psum()
                            for kc in range(8):
                                w.mm(pq, sq_[:, kc, :], hn[:, kc, c0:c0 + 512], start=(kc == 0), stop=(kc == 7))
                            w.tt(qt[:, h, :], pq, Lb[:, i, :], ALU.mult)
                    for i, h in enumerate(hs):
                        ptb = P.psum(BF16)
                        ptb4 = ptb[:, 0:512].rearrange("p (a b) -> p a b", a=4)
                        for tl in range(4):
                            w.tr(ptb4[:, tl, :], khat[:, i, tl * 128:(tl + 1) * 128], identb)
                        w.copy("act", kTt[:, :, h * 128:(h + 1) * 128], ptb4)
                if sbk == 0 and mode == "full":
                    dump("qt_%d" % m, qt, BF16); dump("kt_%d" % m, kt, BF16)
                    dump("kTt_%d" % m, kTt, BF16); dump("itok_%d" % m, itok, BF16); dump("dec_%d" % m, dec)
                P.stage = "%d:L1scan" % m
                for pr in range(4):
                    if mode == "state":
                        pds = [P.psum(), P.psum()]
                        for h in range(8):
                            pv = pds[h // 4].rearrange("p (a b) -> p a b", a=4)
                            w.mm(pv[:, h % 4, :], kTt[:, pr, h * 128:(h + 1) * 128],
                                 itok[:, pr, h * 128:(h + 1) * 128], skip=True)
                        w.tt(S, S, dec[:, :, pr:pr + 1].to_broadcast([128, 8, 128]), ALU.mult)
                        for b_ in range(2):
                            pv = pds[b_].rearrange("p (a b) -> p a b", a=4)
                            w.tt(S[:, b_ * 4:(b_ + 1) * 4, :], S[:, b_ * 4:(b_ + 1) * 4, :], pv, ALU.add)
                        continue
                    tc = slice(pr * 128, (pr + 1) * 128)
                    psc = [P.psum(), P.psum()]
                    for h in range(8):
                        pv = psc[h // 4].rearrange("p (a b) -> p a b", a=4)
                        w.mm(pv[:, h % 4, :], kt[:, h, tc], qt[:, h, tc], skip=True)
                    for b_ in range(2):
                        pv = psc[b_].rearrange("p (a b) -> p a b", a=4)
                        w.tt(scb[:, b_ * 4:(b_ + 1) * 4, :], pv, maskb.unsqueeze(1).to_broadcast([128, 4, 128]), ALU.mult)
                    pso = [P.psum(), P.psum()]
                    for half in range(2):
                        hc = slice(half * 64, (half + 1) * 64)
                        pds = [P.psum(), P.psum()]
                        for h in range(8):
                            pv = pds[h // 4].rearrange("p (a b) -> p a b", a=4)
                            w.mm(pv[:, h % 4, :], kTt[hc, pr, h * 128:(h + 1) * 128], itok[hc, pr, h * 128:(h + 1) * 128],
                                 skip=True)
                        if half == 0:
                            for h in range(8):
                                pv = pso[h // 4].rearrange("p (a b) -> p a b", a=4)
                                w.mm(pv[:, h % 4, :], itok[:, pr, h * 128:(h + 1) * 128], scb[:, h, :],
                                     start=(h % 4 == 0), stop=False, skip=True)
                        for h in range(8):
                            pv = pso[h // 4].rearrange("p (a b) -> p a b", a=4)
                            w.mm(pv[:, h % 4, hc], Sbf[:, h, :], qt[:, h, pr * 128 + half * 64:pr * 128 + (half + 1) * 64],
                                 start=False, stop=(half == 1), skip=True)
                        ci = pr * 2 + half
                        w.tt(S, S, dec[:, :, ci:ci + 1].to_broadcast([128, 8, 128]), ALU.mult)
                        for b_ in range(2):
                            pv = pds[b_].rearrange("p (a b) -> p a b", a=4)
                            w.tt(S[:, b_ * 4:(b_ + 1) * 4, :], S[:, b_ * 4:(b_ + 1) * 4, :], pv, ALU.add)
                        w.copy("act", Sbf, S)
                    for b_ in range(2):
                        pv = pso[b_].rearrange("p (a b) -> p a b", a=4)
                        w.copy("act", o32[:, b_ * 4:(b_ + 1) * 4, pr * 128:(pr + 1) * 128], pv)
                if mode == "state":
                    continue
                P.stage = "%d:L1gate" % m
                SQ = kt
                w.act(SQ, o32, AF.Square)
                for h in range(8):
                    pn = P.psum()
                    w.mm(pn, onesb, SQ[:, h, :])
                    w.act(FL[:, h, :], pn, AF.Ln, bias=EPS, scale=1.0 / 128)
                w.act(FL, FL, AF.Exp, scale=-0.5)
                w.tt(FL, FL, o32, ALU.mult)
                for h in range(8):
                    w.stt(og[:, h, cs], FL[:, h, :], GNG[:, h:h + 1], SG[:, h, :], ALU.mult, ALU.mult)
            if mode == "state":
                if m == last_state:
                    w.ts(S, S, sflag[:, 0:1], None, ALU.mult)
                    w.copy("act", Sbf, S)
                continue
            dump("og_%d" % m, og, BF16)
            P.stage = "%d:L1out" % m
            for mo in range(8):
                so = wload(hg_out[:, mo * 128:(mo + 1) * 128], 8)
                for sbk in range(2):
                    cs = slice(sbk * 512, (sbk + 1) * 512)
                    pt = P.psum()
                    for kc in range(8):
                        w.mm(pt, so[:, kc, :], og[:, kc, cs], start=(kc == 0), stop=(kc == 7))
                    w.tt(hT[:, mo, cs], hT[:, mo, cs], pt, ALU.add)
            dump("hT3_%d" % m, hT)
            if stop == "mix1":
                continue
            P.stage = "%d:ffn1" % m
            ffn(1)
            dump("hT4_%d" % m, hT)

            P.stage = "%d:final" % m
            salloc_reset()
            yT = P.buf([8, 512], F32)
            yst = [P.buf([D], F32) for _ in range(2)]
            for sbk in range(2):
                cs = slice(sbk * 512, (sbk + 1) * 512)
                rmsnorm(hT[:, :, cs], yT, 512, G_FIN)
                for tl in range(4):
                    ys = yst[tl % 2]
                    for half in range(2):
                        pt = P.psum()
                        pt4 = pt.rearrange("p (a b) -> p a b", a=4)
                        for c in range(4):
                            w.tr(pt4[:, c, :], yT[:, half * 4 + c, tl * 128:(tl + 1) * 128], ident)
                        w.copy("act" if half == 0 else "dve", ys[:, half * 512:(half + 1) * 512], pt)
                    r0 = orow * T + sbk * 512 + tl * 128
                    finals.append(w.dma("sp", y[r0:r0 + 128, :], ys))
        with nc.allow_low_precision("bf16 matmul operands, fp32 accumulate"):
            P.emit(final_dmas=finals)
    nc._inames = P.inames
    return nc, dumps


def _pack_params(inp):
    pk = lambda v, n: np.ascontiguousarray(np.asarray(v, np.float32).reshape(n, 128).T)
    par = np.zeros((128, NPAR), np.float32)
    par[:, 0:16] = pk(inp["norm_mix_g"], 16)
    par[:, 16:32] = pk(inp["norm_ffn_g"], 16)
    par[:, 32:40] = pk(inp["final_g"], 8)
    par[:, 40:44] = pk(inp["cp_dw_b"][0], 4)
    par[:, 44:48] = pk(inp["cp_ln_g"][0], 4)
    par[:, 48:52] = pk(inp["cp_ln_b"][0], 4)
    par[:, 52:56] = pk(inp["cp_pool_scale"][0], 4)
    par[:, 56:64] = pk(inp["hg_gn_g"][0], 8)
    par[:, 64:72] = pk(inp["hg_lb_logits"][0], 8)
    par[:, 72:80] = pk(inp["hg_lb_logits"][1], 8)
    dw = np.asarray(inp["cp_dw_w"][0], np.float32)
    par[:, 80:204] = np.transpose(dw.reshape(31, 4, 128), (2, 1, 0)).reshape(128, 124)
    return par


def _invcnt(first_tok_list):
    n = len(first_tok_list)
    inv = np.zeros((128, n, 4, 16), np.float32)
    for m, t0 in enumerate(first_tok_list):
        for g in range(4):
            wd = 2 ** (g + 1)
            tpos = t0 + np.arange(16)
            inv[:, m, g, :] = 1.0 / np.minimum(tpos + 1, wd)
    return inv.reshape(128, -1)


_NC_CACHE = {}
PASSES8 = [("state", None), ("state", None), ("full", 0), ("full", 1)]


def _segments(xseq, t0):
    seg = np.zeros((XPAD + T, D), np.float32)
    lo = t0 - XPAD
    if lo < 0:
        seg[-lo:] = xseq[0:t0 + T]
    else:
        seg[:] = xseq[lo:t0 + T]
    return seg


def make_in_maps(inp, plan):
    par = _pack_params(inp)
    x = np.asarray(inp["x"], np.float32)
    common = {
        "par": par,
        "cp_w_in": np.ascontiguousarray(inp["cp_w_in"][0]), "cp_pool_w": np.ascontiguousarray(inp["cp_pool_w"][0]),
        "cp_w_out": np.ascontiguousarray(inp["cp_w_out"][0]), "hg_w_in": np.ascontiguousarray(inp["hg_w_in"][0]),
        "hg_w_out": np.ascontiguousarray(inp["hg_w_out"][0]),
        "ffn_w1": np.ascontiguousarray(inp["ffn_w1"]), "ffn_w3": np.ascontiguousarray(inp["ffn_w3"]),
        "ffn_w2": np.ascontiguousarray(inp["ffn_w2"]),
    }
    maps = []
    for b, t0s, flag in plan:
        d = dict(common)
        d["xp"] = np.concatenate([_segments(x[b], t0) for t0 in t0s], axis=0)
        d["invcnt"] = _invcnt(t0s)
        d["sflag"] = np.full((128, 1), flag, np.float32)
        maps.append(d)
    return maps


def plan8():
    plan = []
    for b in range(4):
        plan.append((b, [0, T, 0, T], 0.0))
        plan.append((b, [0, T, 2 * T, 3 * T], 1.0))
    return plan


def kernel(**inputs):
    inp = {k: np.asarray(v) for k, v in inputs.items()}
    if "nc" not in _NC_CACHE:
        _NC_CACHE["nc"] = build_program(passes=PASSES8)[0]
    nc = _NC_CACHE["nc"]
    maps = make_in_maps(inp, plan8())
    res = run_bass_kernel_spmd(nc, maps, core_ids=list(range(8)))
    out = np.stack([np.asarray(r["y"], np.float32) for r in res.results], axis=0)
    return out.reshape(4, SEQ, D)
```

```python
import numpy as np
import concourse.bass as bass
import concourse.mybir as mybir

F32 = mybir.dt.float32
BF16 = mybir.dt.bfloat16
ALU = mybir.AluOpType
AF = mybir.ActivationFunctionType

ENGS = ["pe", "act", "dve", "pool", "sp"]
EIDX = {e: i for i, e in enumerate(ENGS)}
GR = 64
NDMASEM = 8


def _esz(dt):
    return 2 if dt == BF16 else 4


class Space:
    def __init__(self, nbytes):
        n = (nbytes + GR - 1) // GR
        self.lw = np.full(n, -1, np.int64)
        self.rd = np.full((len(ENGS), n), -1, np.int64)
        self.dmard = {}


class Prog:
    def __init__(self, nc, sb, ps, sb_bytes, ps_bytes):
        self.nc = nc
        self.sb = sb
        self.ps = ps
        self.spaces = {"sb": Space(sb_bytes), "ps": Space(ps_bytes)}
        self.ops = []
        self.sb_off = 0
        self.sb_bytes = sb_bytes
        self.psbank = 0
        self.ndma = {e: 0 for e in ENGS}
        self.ncc = 0
        self.inames = {}
        self.stage = ""

    def alloc(self, nbytes, at=None):
        nbytes = (nbytes + GR - 1) // GR * GR
        if at is None:
            at = self.sb_off
            self.sb_off += nbytes
            assert self.sb_off <= self.sb_bytes, (self.sb_off, self.sb_bytes)
        return at

    def view(self, off, shape, dt):
        n = int(np.prod(shape))
        es = _esz(dt)
        assert off % 4 == 0
        w = (n * es + 3) // 4
        ap = self.sb[:, off // 4: off // 4 + w]
        if dt != F32:
            ap = ap.bitcast(dt)
            ap = ap[:, 0:n]
        if len(shape) == 2:
            ap = ap.rearrange("p (a b) -> p a b", a=shape[0])
        elif len(shape) == 3:
            ap = ap.rearrange("p (a b c) -> p a b c", a=shape[0], b=shape[1])
        return ap

    def buf(self, shape, dt, at=None):
        n = int(np.prod(shape)) * _esz(dt)
        off = self.alloc(n, at)
        return self.view(off, shape, dt)

    def psum(self, dt=F32, bank=None):
        if bank is None:
            bank = self.psbank
            self.psbank = (self.psbank + 1) % 8
        ap = self.ps[:, bank * 512:(bank + 1) * 512]
        if dt != F32:
            ap = ap.bitcast(dt)
        return ap

    @staticmethod
    def extent(ap):
        apl = ap.ap
        pstep = apl[0][0]
        off = int(ap.offset)
        es = _esz(ap.dtype)
        lo = off % pstep if pstep > 0 else off
        hi = lo + 1
        for st, cnt in apl[1:]:
            hi += abs(st) * (cnt - 1)
        space = "ps" if "PSUM" in str(ap.space).upper() or "PSUM" in type(ap.tensor).__name__.upper() else "sb"
        return space, lo * es, hi * es

    def op(self, eng, fn, reads=(), writes=(), dma=False, name="", extra=(), cc=False):
        oid = len(self.ops)
        ei = EIDX[eng]
        deps = set(extra)
        racc = [self.extent(a) for a in reads if a is not None and not self._isdram(a)]
        wacc = [self.extent(a) for a in writes if a is not None and not self._isdram(a)]
        for sp, lo, hi in racc:
            s = self.spaces[sp]
            g0, g1 = lo // GR, (hi + GR - 1) // GR
            for w in np.unique(s.lw[g0:g1]):
                if w >= 0:
                    deps.add(int(w))
        for sp, lo, hi in wacc:
            s = self.spaces[sp]
            g0, g1 = lo // GR, (hi + GR - 1) // GR
            for w in np.unique(s.lw[g0:g1]):
                if w >= 0:
                    deps.add(int(w))
            m = s.rd[:, g0:g1].max(axis=1)
            for v in m:
                if v >= 0:
                    deps.add(int(v))
            if s.dmard:
                for g in range(g0, g1):
                    st = s.dmard.pop(g, None)
                    if st:
                        deps.update(st)
        for sp, lo, hi in racc:
            s = self.spaces[sp]
            g0, g1 = lo // GR, (hi + GR - 1) // GR
            if dma:
                for g in range(g0, g1):
                    s.dmard.setdefault(g, set()).add(oid)
            else:
                np.maximum(s.rd[ei, g0:g1], oid, out=s.rd[ei, g0:g1])
        for sp, lo, hi in wacc:
            s = self.spaces[sp]
            g0, g1 = lo // GR, (hi + GR - 1) // GR
            s.lw[g0:g1] = oid
            s.rd[:, g0:g1] = -1
        deps.discard(oid)
        o = dict(id=oid, eng=eng, fn=fn, deps=deps, dma=dma, name=name, stage=getattr(self, "stage", ""))
        if cc:
            o["cc"] = self.ncc
            self.ncc += 1
        elif dma:
            o["dman"] = self.ndma[eng]
            self.ndma[eng] += 1
        self.ops.append(o)
        return oid

    @staticmethod
    def _isdram(ap):
        return "DRam" in type(ap.tensor).__name__

    def emit(self, final_dmas=()):
        nc = self.nc
        ops = self.ops
        fid = len(ops)
        ops.append(dict(id=fid, eng="sp", fn=None, deps=set(final_dmas), dma=False, name="final"))
        signaling = set()
        for o in ops:
            for d in o["deps"]:
                if not ops[d]["dma"]:
                    if ops[d]["eng"] == "pe" and o["eng"] == "pe":
                        continue
                    signaling.add(d)
        sigidx = {}
        cnt = {e: 0 for e in ENGS}
        for o in ops:
            if o["id"] in signaling:
                cnt[o["eng"]] += 1
                sigidx[o["id"]] = cnt[o["eng"]]
        self.sigcount = dict(cnt)
        import contextlib
        with contextlib.ExitStack() as st:
            esem = {e: st.enter_context(nc.semaphore("s_" + e)) for e in ENGS}
            dsem = {e: [st.enter_context(nc.semaphore("d_%s%d" % (e, i))) for i in range(NDMASEM)]
                    for e in ENGS if self.ndma[e] > 0}
            ccsem = [st.enter_context(nc.semaphore("cc%d" % i)) for i in range(self.ncc)]
            block = st.enter_context(nc.Block())

            def stream(ename):
                def run(e):
                    waited = {}
                    for o in ops:
                        if o["eng"] != ename:
                            continue
                        need = {}
                        for d in o["deps"]:
                            od = ops[d]
                            if "cc" in od:
                                key = ("c", "cc", od["cc"])
                                val = 1
                            elif od["dma"]:
                                n = od["dman"]
                                key = ("d", od["eng"], n % NDMASEM)
                                val = 16 * (n // NDMASEM + 1)
                            else:
                                if od["eng"] == "pe" and ename == "pe":
                                    continue
                                key = ("e", od["eng"])
                                val = sigidx[d]
                            if need.get(key, 0) < val:
                                need[key] = val
                        if o["dma"] and "cc" not in o:
                            n = o["dman"]
                            if n >= NDMASEM:
                                key = ("d", ename, n % NDMASEM)
                                val = 16 * (n // NDMASEM)
                                if need.get(key, 0) < val:
                                    need[key] = val
                        for key, val in need.items():
                            if waited.get(key, 0) >= val:
                                continue
                            waited[key] = val
                            sem = esem[key[1]] if key[0] == "e" else (ccsem[key[2]] if key[0] == "c" else dsem[key[1]][key[2]])
                            e.wait_ge(sem, val)
                        if o["fn"] is None:
                            continue
                        ins = o["fn"](e)
                        try:
                            self.inames[ins.ins.name] = o["stage"]
                        except Exception:
                            pass
                        if "cc" in o:
                            ins.then_inc(ccsem[o["cc"]])
                        elif o["dma"]:
                            ins.then_inc(dsem[ename][o["dman"] % NDMASEM], 16)
                        elif o["id"] in signaling:
                            ins.then_inc(esem[ename], 1)
                return run

            block.tensor(stream("pe"))
            block.scalar(stream("act"))
            block.vector(stream("dve"))
            block.gpsimd(stream("pool"))
            block.sync(stream("sp"))


from concourse.bass_utils import run_bass_kernel_spmd

D = 1024
SEQ = 4096
KC = 8
T = 1024
HL = 32
XPAD = 128
DFF = 2816
NJ = DFF // 128
EPS = 1e-6
NPAR = 208


class W:
    def __init__(self, P):
        self.P = P

    def mm(self, out, lhsT, rhs, start=True, stop=True, skip=False):
        self.P.op("pe", lambda e: e.matmul(out, lhsT, rhs, start=start, stop=stop, skip_group_check=skip),
                  reads=[lhsT, rhs], writes=[out])

    def tr(self, out, in_, ident):
        self.P.op("pe", lambda e: e.transpose(out, in_, ident), reads=[in_, ident], writes=[out])

    def act(self, out, in_, func, bias=None, scale=None, eng="act"):
        kw = {}
        rd = [in_]
        if bias is not None:
            kw["bias"] = bias
            if not isinstance(bias, float):
                rd.append(bias)
        if scale is not None:
            kw["scale"] = scale
            if not isinstance(scale, float):
                rd.append(scale)
        self.P.op("act", lambda e: e.activation(out, in_, func, **kw), reads=rd, writes=[out])

    def tt(self, out, a, b, op, eng="dve"):
        self.P.op(eng, lambda e: e.tensor_tensor(out, a, b, op), reads=[a, b], writes=[out])

    def ts(self, out, a, s1, s2, op0, op1=None, eng="dve"):
        rd = [a] + [s for s in (s1, s2) if s is not None and not isinstance(s, (int, float))]
        if op1 is None:
            self.P.op(eng, lambda e: e.tensor_single_scalar(out, a, s1, op0), reads=rd, writes=[out])
        else:
            self.P.op(eng, lambda e: e.tensor_scalar(out, a, s1, s2, op0, op1), reads=rd, writes=[out])

    def stt(self, out, a, s, b, op0, op1):
        rd = [a, b] + ([] if isinstance(s, (int, float)) else [s])
        self.P.op("dve", lambda e: e.scalar_tensor_tensor(out, a, s, b, op0, op1), reads=rd, writes=[out])

    def copy(self, eng, out, in_):
        if eng == "act":
            self.P.op("act", lambda e: e.copy(out, in_), reads=[in_], writes=[out])
        else:
            self.P.op(eng, lambda e: e.tensor_copy(out, in_), reads=[in_], writes=[out])

    def dma(self, eng, out, in_):
        return self.P.op(eng, lambda e: e.dma_start(out=out, in_=in_), reads=[in_], writes=[out], dma=True)


def build_program(passes=None, dbg=None, stop=None, nmb=None):
    dbg = dbg or []
    if passes is None:
        passes = [("full", i) for i in range(nmb or 4)]
    npass = len(passes)
    nfull = sum(1 for p in passes if p[0] == "full")
    last_state = max([i for i, p in enumerate(passes) if p[0] == "state"], default=-1)
    nc = bass.Bass("TRN2", target_bir_lowering=False)
    dt_in = lambda name, shape: nc.dram_tensor(name, list(shape), F32, kind="ExternalInput").ap()
    xp = dt_in("xp", [npass * (XPAD + T), D])
    sflag_d = dt_in("sflag", [128, 1])
    par_d = dt_in("par", [128, NPAR])
    inv_d = dt_in("invcnt", [128, npass * 4 * 16])
    w_in0 = dt_in("cp_w_in", [D, 1536])
    pool_w = dt_in("cp_pool_w", [4, 128, 128])
    w_out0 = dt_in("cp_w_out", [D, D])
    hg_in = dt_in("hg_w_in", [D, 4096])
    hg_out = dt_in("hg_w_out", [D, D])
    w1 = dt_in("ffn_w1", [2, D, DFF])
    w3 = dt_in("ffn_w3", [2, D, DFF])
    w2 = dt_in("ffn_w2", [2, DFF, D])
    y = nc.dram_tensor("y", [nfull * T, D], F32, kind="ExternalOutput").ap()
    dumps = {}
    finals = []

    SBB = 206 * 1024
    import contextlib
    with contextlib.ExitStack() as stack:
        sb = stack.enter_context(nc.sbuf_tensor("SB", [128, SBB // 4], F32))
        ps = stack.enter_context(nc.psum_tensor("PS", [128, 4096], F32))
        P = Prog(nc, sb, ps, SBB, 16384)
        w = W(P)

        def dump(name, ap, dt=F32):
            if name not in dbg:
                return
            t = nc.dram_tensor("dbg_" + name, list(ap.shape), dt, kind="ExternalOutput").ap()
            finals.append(w.dma("sp", t, ap))
            dumps[name] = t

        hT = P.buf([8, T], F32)
        hTh = P.buf([8, HL], F32)
        hn = P.buf([8, HL + T], BF16)
        S = P.buf([8, 128], F32)
        Sbf = P.buf([8, 128], BF16)
        par = P.buf([NPAR], F32)
        inv = P.buf([npass, 4, 16], F32)
        sflag = P.buf([1], F32)
        ident = P.buf([128], F32)
        identb = P.buf([128], BF16)
        onesb = P.buf([128], BF16)
        onesf = P.buf([128], F32)
        ones512 = P.buf([512], F32)
        maskb = P.buf([128], F32)
        lb = P.buf([8], F32)
        oml = P.buf([8], F32)
        noml = P.buf([8], F32)
        rstd = P.buf([512], F32)
        sqr = [P.buf([512], BF16) for _ in range(4)]
        tmpf = [P.buf([512], F32) for _ in range(4)]
        ring8 = [P.buf([8, 128], BF16) for _ in range(8)]
        ring22 = [P.buf([NJ, 128], BF16) for _ in range(2)]
        ring1 = [P.buf([1, 128], BF16) for _ in range(2)]
        rstate = {"r8": 0, "r22": 0, "r1": 0, "sq": 0, "tf": 0}
        stage0 = P.sb_off
        stage_bytes = SBB - stage0

        def salloc_reset():
            P.sb_off = stage0

        def wload(wv, kc):
            if kc == 8:
                slot = ring8[rstate["r8"] % len(ring8)]; rstate["r8"] += 1
            elif kc == NJ:
                slot = ring22[rstate["r22"] % len(ring22)]; rstate["r22"] += 1
            else:
                slot = ring1[rstate["r1"] % len(ring1)]; rstate["r1"] += 1
            w.dma("pool", slot, wv.rearrange("(kc p) m -> p kc m", p=128))
            return slot

        def sqtmp():
            r = sqr[rstate["sq"] % 4]; rstate["sq"] += 1
            return r

        def ftmp():
            r = tmpf[rstate["tf"] % 4]; rstate["tf"] += 1
            return r

        P.op("pool", lambda e: e.memset(onesf, 1.0), writes=[onesf])
        P.op("pool", lambda e: e.memset(ones512, 1.0), writes=[ones512])
        P.op("pool", lambda e: e.affine_select(ident, onesf, pattern=[[-1, 128]], compare_op=ALU.is_equal,
                                                 fill=0.0, base=0, channel_multiplier=1),
             reads=[onesf], writes=[ident])
        w.copy("dve", identb, ident)
        w.copy("dve", onesb, onesf)
        P.op("pool", lambda e: e.affine_select(maskb, onesf, pattern=[[1, 128]], compare_op=ALU.is_ge,
                                                 fill=0.0, base=0, channel_multiplier=-1),
             reads=[onesf], writes=[maskb])
        P.op("pool", lambda e: e.memset(maskb[0:64, 64:128], 0.0), writes=[maskb[0:64, 64:128]])
        w.dma("sp", par, par_d)
        w.dma("sp", sflag, sflag_d)
        w.dma("sp", inv.rearrange("p a b c -> p (a b c)"), inv_d)
        G_MIX = lambda l: par[:, l * 8:(l + 1) * 8]
        G_FFN = lambda l: par[:, 16 + l * 8:16 + (l + 1) * 8]
        G_FIN = par[:, 32:40]
        DWB = par[:, 40:44]; LNG = par[:, 44:48]; LNB = par[:, 48:52]; PSC = par[:, 52:56]
        GNG = par[:, 56:64]
        DWW = par[:, 80:204].rearrange("p (j k) -> p j k", k=31)
        w.tt(lb, par[:, 72:80], par[:, 64:72], ALU.subtract)
        w.act(lb, lb, AF.Sigmoid)
        w.ts(oml, lb, -1.0, 1.0, ALU.mult, ALU.add)
        w.ts(noml, oml, -1.0, None, ALU.mult)
        P.op("dve", lambda e: e.memset(S, 0.0), writes=[S])
        P.op("dve", lambda e: e.memset(Sbf, 0.0), writes=[Sbf])

        def rmsnorm(src, dst, n, gain, nch=8):
            pt = P.psum()
            for c in range(nch):
                sq = sqtmp()
                w.act(sq[:, 0:n], src[:, c, :], AF.Square)
                w.mm(pt[:, 0:n], onesb, sq[:, 0:n], start=(c == 0), stop=(c == nch - 1))
            w.act(rstd[:, 0:n], pt[:, 0:n], AF.Ln, bias=EPS, scale=1.0 / (nch * 128))
            w.act(rstd[:, 0:n], rstd[:, 0:n], AF.Exp, scale=-0.5)
            for c in range(nch):
                w.stt(dst[:, c, :], src[:, c, :], gain[:, c:c + 1], rstd[:, 0:n], ALU.mult, ALU.mult)

        def ffn(l):
            salloc_reset()
            h1 = P.buf([NJ, T], BF16)
            for sbk in range(2):
                cs = slice(sbk * 512, (sbk + 1) * 512)
                rmsnorm(hT[:, :, cs], hn[:, :, HL + sbk * 512:HL + (sbk + 1) * 512], 512, G_FFN(l))
            for j in range(NJ):
                sa = wload(w1[l][:, j * 128:(j + 1) * 128], 8)
                sb_ = wload(w3[l][:, j * 128:(j + 1) * 128], 8)
                for sbk in range(2):
                    c0 = HL + sbk * 512
                    pa = P.psum(); pb = P.psum()
                    for kc in range(8):
                        w.mm(pa, sa[:, kc, :], hn[:, kc, c0:c0 + 512], start=(kc == 0), stop=(kc == 7))
                    for kc in range(8):
                        w.mm(pb, sb_[:, kc, :], hn[:, kc, c0:c0 + 512], start=(kc == 0), stop=(kc == 7))
                    t = ftmp()
                    w.act(t, pa, AF.Silu)
                    w.tt(h1[:, j, sbk * 512:(sbk + 1) * 512], pb, t, ALU.mult)
            for mo in range(8):
                s2 = wload(w2[l][:, mo * 128:(mo + 1) * 128], NJ)
                for sbk in range(2):
                    cs = slice(sbk * 512, (sbk + 1) * 512)
                    pt = P.psum()
                    for j in range(NJ):
                        w.mm(pt, s2[:, j, :], h1[:, j, cs], start=(j == 0), stop=(j == NJ - 1))
                    w.tt(hT[:, mo, cs], hT[:, mo, cs], pt, ALU.add)

        for m, (mode, orow) in enumerate(passes):
            P.stage = "%d:load" % m
            salloc_reset()
            xs = [P.buf([D], F32, at=stage0 + 32 * 1024 + i * 4096) for i in range(2)]
            for tt in range(-1, T // 128):
                xt = xs[(tt + 1) % 2]
                r0 = m * (XPAD + T) + XPAD + tt * 128
                w.dma("sp", xt, xp[r0:r0 + 128, :])
                for half in range(2):
                    pt = P.psum()
                    pt4 = pt.rearrange("p (a b) -> p a b", a=4)
                    for c in range(4):
                        cc = half * 4 + c
                        w.tr(pt4[:, c, :], xt[:, cc * 128:(cc + 1) * 128], ident)
                    if tt < 0:
                        w.copy("act", hTh[:, half * 4:(half + 1) * 4, :], pt4[:, :, 128 - HL:128])
                    else:
                        w.copy("act", hT[:, half * 4:(half + 1) * 4, tt * 128:(tt + 1) * 128], pt4)
            dump("hT0_%d" % m, hT)

            P.stage = "%d:L0in" % m
            salloc_reset()
            a_bf = P.buf([4, HL + T], BF16)
            u32 = P.buf([4, HL + T], F32)
            cv = P.buf([4, 512], F32)
            cvb = P.buf([4, 512], BF16)
            cvq = P.buf([4, 512], BF16)
            tA = P.buf([HL + T], F32)
            tB = P.buf([HL + T], F32)
            d_bf = P.buf([4, T], BF16)
            cat = P.buf([8, T], BF16)
            diag = [P.buf([31, 128], BF16) for _ in range(2)]
            mu = P.buf([512], F32)
            msq = P.buf([512], F32)
            rs2 = P.buf([512], F32)
            t16 = P.buf([16], F32)

            rmsnorm(hTh, hn[:, :, 0:HL], HL, G_MIX(0))
            for sbk in range(2):
                cs = slice(sbk * 512, (sbk + 1) * 512)
                rmsnorm(hT[:, :, cs], hn[:, :, HL + sbk * 512:HL + (sbk + 1) * 512], 512, G_MIX(0))
            dump("hn0_%d" % m, hn, BF16)
            blocks = [(0, HL), (HL, 512), (HL + 512, 512)]
            for j in range(4):
                sg = wload(w_in0[:, 512 + j * 128:512 + (j + 1) * 128], 8)
                sv = wload(w_in0[:, j * 128:(j + 1) * 128], 8)
                for c0, n in blocks:
                    pa = P.psum(); pb = P.psum()
                    for kc in range(8):
                        w.mm(pa[:, 0:n], sg[:, kc, :], hn[:, kc, c0:c0 + n], start=(kc == 0), stop=(kc == 7))
                    for kc in range(8):
                        w.mm(pb[:, 0:n], sv[:, kc, :], hn[:, kc, c0:c0 + n], start=(kc == 0), stop=(kc == 7))
                    t = ftmp()
                    w.act(t[:, 0:n], pa[:, 0:n], AF.Sigmoid)
                    w.tt(a_bf[:, j, c0:c0 + n], pb[:, 0:n], t[:, 0:n], ALU.mult)
            for j in range(4):
                su = wload(w_in0[:, 1024 + j * 128:1024 + (j + 1) * 128], 8)
                for c0, n in blocks:
                    pa = P.psum()
                    for kc in range(8):
                        w.mm(pa[:, 0:n], su[:, kc, :], hn[:, kc, c0:c0 + n], start=(kc == 0), stop=(kc == 7))
                    w.copy("act", u32[:, j, c0:c0 + n], pa[:, 0:n])
            dump("a_%d" % m, a_bf, BF16)
            dump("u_%d" % m, u32)
            def pool_group(g):
                    src = u32[:, g, :]
                    lo = 0
                    bufs = [tA, tB]
                    bi = 0
                    for st in [1, 2, 4, 8][:g + 1]:
                        lo += st
                        dst = bufs[bi]; bi ^= 1
                        w.tt(dst[:, lo:HL + T], src[:, lo:HL + T], src[:, lo - st:HL + T - st], ALU.add)
                        src = dst
                    wd = float(2 ** (g + 1))
                    w.stt(d_bf[:, g, :], src[:, HL:HL + T], 1.0 / wd, u32[:, g, HL:HL + T], ALU.mult, ALU.subtract)
                    w.tt(t16, src[:, HL:HL + 16], inv[:, m, g, :], ALU.mult)
                    w.tt(d_bf[:, g, 0:16], t16, u32[:, g, HL:HL + 16], ALU.subtract)
            pass
            P.stage = "%d:L0conv" % m
            def build_diag(idx):
                j = idx % 4
                dg = diag[idx % 2]
                P.op("dve", lambda e: e.tensor_tensor(
                    dg, ident.unsqueeze(1).to_broadcast([128, 31, 128]),
                    DWW[:, j, :].unsqueeze(2).to_broadcast([128, 31, 128]), ALU.mult),
                    reads=[ident, DWW[:, j, :]], writes=[dg])
            build_diag(0)
            pst = {}

            def conv_stats(idx):
                sb_, j_ = idx // 4, idx % 4
                if j_ == 0:
                    pst[sb_] = (P.psum(), P.psum())
                p1, p2 = pst[sb_]
                w.mm(p1, onesb, cvb[:, j_, :], start=(j_ == 0), stop=(j_ == 3))
                w.mm(p2, onesb, cvq[:, j_, :], start=(j_ == 0), stop=(j_ == 3))
                if j_ == 3:
                    csl = slice(sb_ * 512, (sb_ + 1) * 512)
                    w.ts(mu, p1, 1.0 / 512, None, ALU.mult)
                    w.tt(msq, mu, mu, ALU.mult)
                    w.stt(rs2, p2, 1.0 / 512, msq, ALU.mult, ALU.subtract)
                    w.act(rs2, rs2, AF.Ln, bias=EPS)
                    w.act(rs2, rs2, AF.Exp, scale=-0.5)
                    for jj in range(4):
                        t = ftmp()
                        w.tt(t, cv[:, jj, :], mu, ALU.subtract)
                        w.tt(t, t, rs2, ALU.mult)
                        w.act(cat[:, jj, csl], t, AF.Silu, bias=LNB[:, jj:jj + 1], scale=LNG[:, jj:jj + 1])

            for idx in range(8):
                sbk, j = idx // 4, idx % 4
                dg = diag[idx % 2]
                pc = P.psum()
                for k in range(31):
                    o0 = HL + sbk * 512 - (30 - k)
                    w.mm(pc, dg[:, k, :], a_bf[:, j, o0:o0 + 512], start=(k == 0), stop=(k == 30))
                if idx + 1 < 8:
                    build_diag(idx + 1)
                if idx > 0:
                    conv_stats(idx - 1)
                w.act(cv[:, j, :], pc, AF.Identity, bias=DWB[:, j:j + 1])
                w.copy("dve", cvb[:, j, :], cv[:, j, :])
                w.act(cvq[:, j, :], cv[:, j, :], AF.Square)
                if sbk == 0:
                    pool_group(j)
            conv_stats(7)
            for g in range(4):
                sp_ = wload(pool_w[g], 1)
                for sbk in range(2):
                    cs = slice(sbk * 512, (sbk + 1) * 512)
                    pt = P.psum()
                    w.mm(pt, sp_[:, 0, :], d_bf[:, g, cs])
                    w.act(cat[:, 4 + g, cs], pt, AF.Identity, scale=PSC[:, g:g + 1])
            dump("cat_%d" % m, cat, BF16)
            P.stage = "%d:L0out" % m
            for mo in range(8):
                so = wload(w_out0[:, mo * 128:(mo + 1) * 128], 8)
                for sbk in range(2):
                    cs = slice(sbk * 512, (sbk + 1) * 512)
                    pt = P.psum()
                    for kc in range(8):
                        w.mm(pt, so[:, kc, :], cat[:, kc, cs], start=(kc == 0), stop=(kc == 7))
                    w.tt(hT[:, mo, cs], hT[:, mo, cs], pt, ALU.add)
            dump("hT1_%d" % m, hT)
            if stop == "mix0":
                continue
            P.stage = "%d:ffn0" % m
            ffn(0)
            dump("hT2_%d" % m, hT)
            if stop == "ffn0":
                continue

            if mode == "state":
                P.stage = "%d:L1st" % m
                salloc_reset()
                Fs = [P.buf([8, 512], F32) for _ in range(2)]
                Ls = P.buf([8, 512], F32)
                Cs_ = P.buf([8, 520], F32)
                khs = P.buf([8, 512], BF16)
                kTs = P.buf([4, D], BF16)
                its = [P.buf([4, D], BF16) for _ in range(2)]
                decs = [P.buf([8, 4], F32) for _ in range(2)]
                P.op("dve", (lambda cz: lambda e: e.memset(cz, 0.0))(Cs_[:, :, 0:8]), writes=[Cs_[:, :, 0:8]])
                for sbk in range(2):
                    cs = slice(sbk * 512, (sbk + 1) * 512)
                    rmsnorm(hT[:, :, cs], hn[:, :, HL + sbk * 512:HL + (sbk + 1) * 512], 512, G_MIX(1))

                def st_A(sbk):
                    c0 = HL + sbk * 512
                    for h in range(8):
                        sf = wload(hg_in[:, 1024 + h * 128:1024 + (h + 1) * 128], 8)
                        pf = P.psum()
                        for kc in range(8):
                            w.mm(pf, sf[:, kc, :], hn[:, kc, c0:c0 + 512], start=(kc == 0), stop=(kc == 7))
                        w.act(Fs[sbk][:, h, :], pf, AF.Sigmoid)

                def st_I(sbk):
                    c0 = HL + sbk * 512
                    for hv in range(8):
                        si = wload(hg_in[:, 2048 + hv * 128:2048 + (hv + 1) * 128], 8)
                        pt = P.psum()
                        pt4 = pt.rearrange("p (a b) -> p a b", a=4)
                        for tl in range(4):
                            for kc in range(8):
                                w.mm(pt4[:, tl, :], hn[:, kc, c0 + tl * 128:c0 + (tl + 1) * 128], si[:, kc, :],
                                     start=(kc == 0), stop=(kc == 7), skip=True)
                        w.copy("dve", its[sbk][:, :, hv * 128:(hv + 1) * 128], pt4)

                def st_B(sbk):
                    Fq = Fs[sbk]
                    for h in range(8):
                        w.act(Ls[:, h, :], Fq[:, h, :], AF.Ln, bias=lb[:, h:h + 1], scale=oml[:, h:h + 1])
                    for h in range(8):
                        w.ts(Fq[:, h, :], Fq[:, h, :], noml[:, h:h + 1], oml[:, h:h + 1], ALU.mult, ALU.add)
                    for h in range(8):
                        P.op("dve", (lambda co, li: lambda e: e.tensor_tensor_scan(
                            co, ones512, li, 0.0, ALU.mult, ALU.add))(Cs_[:, h, 8:520], Ls[:, h, :]),
                            reads=[ones512, Ls[:, h, :]], writes=[Cs_[:, h, 8:520]])
                    Cs2 = Cs_[:, :, 7:519].rearrange("p h (c l) -> p h c l", l=128)[:, :, :, 0:1]
                    Ce2 = Cs_[:, :, 8:520].rearrange("p h (c l) -> p h c l", l=128)[:, :, :, 127:128]
                    L2 = Ls.rearrange("p h (c l) -> p h c l", l=128)
                    w.tt(L2, Ce2.to_broadcast([128, 8, 4, 128]),
                         Cs_[:, :, 8:520].rearrange("p h (c l) -> p h c l", l=128), ALU.subtract)
                    w.tt(decs[sbk].unsqueeze(3), Ce2, Cs2, ALU.subtract)
                    w.act(decs[sbk], decs[sbk], AF.Exp)
                    w.act(Ls, Ls, AF.Exp)
                    w.tt(khs, Ls, Fq, ALU.mult)

                def st_C(sbk):
                    for h in range(8):
                        ptb = P.psum(BF16)
                        ptb4 = ptb[:, 0:512].rearrange("p (a b) -> p a b", a=4)
                        for tl in range(4):
                            w.tr(ptb4[:, tl, :], khs[:, h, tl * 128:(tl + 1) * 128], identb)
                        w.copy("act" if h % 2 == 0 else "dve", kTs[:, :, h * 128:(h + 1) * 128], ptb4)

                def st_S(sbk):
                    for pr in range(4):
                        pds = [P.psum(), P.psum()]
                        for h in range(8):
                            pv = pds[h // 4].rearrange("p (a b) -> p a b", a=4)
                            w.mm(pv[:, h % 4, :], kTs[:, pr, h * 128:(h + 1) * 128],
                                 its[sbk][:, pr, h * 128:(h + 1) * 128], skip=True)
                        w.tt(S, S, decs[sbk][:, :, pr:pr + 1].to_broadcast([128, 8, 128]), ALU.mult)
                        for b_ in range(2):
                            pv = pds[b_].rearrange("p (a b) -> p a b", a=4)
                            w.tt(S[:, b_ * 4:(b_ + 1) * 4, :], S[:, b_ * 4:(b_ + 1) * 4, :], pv, ALU.add)

                st_A(0); st_I(0); st_B(0); st_A(1); st_I(1); st_C(0); st_S(0); st_B(1); st_C(1); st_S(1)
                if m == last_state:
                    w.ts(S, S, sflag[:, 0:1], None, ALU.mult)
                    w.copy("act", Sbf, S)
                continue
            P.stage = "%d:L1chain" % m
            salloc_reset()
            GS = 8 if mode == "state" else 4
            NG = 8 // GS
            kTt = P.buf([4, D], BF16)
            itok = P.buf([4, D], BF16)
            if mode == "full":
                qt = P.buf([8, 512], BF16)
                kt = P.buf([8, 512], BF16)
                o32 = P.buf([8, 512], F32)
                og = P.buf([8, T], BF16)
                SG = P.buf([8, 512], BF16)
                FL = P.buf([8, 512], F32)
                Fb = FL[:, 0:4, :]
                Lb = FL[:, 4:8, :]
            else:
                Fb = P.buf([8, 512], F32)
                Lb = P.buf([8, 512], F32)
            Cb = P.buf([GS, 520], F32)
            khat = P.buf([GS, 512], BF16)
            dec = P.buf([8, 8], F32)
            scb = P.buf([8, 128], BF16)
            P.op("dve", (lambda cz: lambda e: e.memset(cz, 0.0))(Cb[:, :, 0:8]), writes=[Cb[:, :, 0:8]])
            for sbk in range(2):
                cs = slice(sbk * 512, (sbk + 1) * 512)
                rmsnorm(hT[:, :, cs], hn[:, :, HL + sbk * 512:HL + (sbk + 1) * 512], 512, G_MIX(1))
            for sbk in range(2):
                c0 = HL + sbk * 512
                cs = slice(sbk * 512, (sbk + 1) * 512)
                P.stage = "%d:L1chain" % m
                def seg_A(hg):
                        hs = list(range(hg * GS, hg * GS + GS))
                        for i, h in enumerate(hs):
                            sf = wload(hg_in[:, 1024 + h * 128:1024 + (h + 1) * 128], 8)
                            pf = P.psum()
                            for kc in range(8):
                                w.mm(pf, sf[:, kc, :], hn[:, kc, c0:c0 + 512], start=(kc == 0), stop=(kc == 7))
                            w.act(Fb[:, i, :], pf, AF.Sigmoid)
                def seg_F(hg):
                        hs = list(range(hg * GS, hg * GS + GS))
                        if hg == 0:
                            for hv in range(8):
                                si = wload(hg_in[:, 2048 + hv * 128:2048 + (hv + 1) * 128], 8)
                                pt = P.psum()
                                pt4 = pt.rearrange("p (a b) -> p a b", a=4)
                                for tl in range(4):
                                    for kc in range(8):
                                        w.mm(pt4[:, tl, :], hn[:, kc, c0 + tl * 128:c0 + (tl + 1) * 128], si[:, kc, :],
                                             start=(kc == 0), stop=(kc == 7), skip=True)
                                w.copy("act", itok[:, :, hv * 128:(hv + 1) * 128], pt4)
                        elif mode == "full":
                            for h in range(8):
                                sgw = wload(hg_in[:, 3072 + h * 128:3072 + (h + 1) * 128], 8)
                                pg = P.psum()
                                for kc in range(8):
                                    w.mm(pg, sgw[:, kc, :], hn[:, kc, c0:c0 + 512], start=(kc == 0), stop=(kc == 7))
                                w.act(SG[:, h, :], pg, AF.Silu)
                def seg_B(hg):
                        hs = list(range(hg * GS, hg * GS + GS))
                        for i, h in enumerate(hs):
                            w.act(Lb[:, i, :], Fb[:, i, :], AF.Ln, bias=lb[:, h:h + 1], scale=oml[:, h:h + 1])
                        for i, h in enumerate(hs):
                            w.ts(Fb[:, i, :], Fb[:, i, :], noml[:, h:h + 1], oml[:, h:h + 1], ALU.mult, ALU.add)
                        for i, h in enumerate(hs):
                            P.op("dve", (lambda co, li: lambda e: e.tensor_tensor_scan(
                                co, ones512, li, 0.0, ALU.mult, ALU.add))(Cb[:, i, 8:520], Lb[:, i, :]),
                                reads=[ones512, Lb[:, i, :]], writes=[Cb[:, i, 8:520]])
                        Cc = Cb[:, :, 8:520].rearrange("p h (c l) -> p h c l", l=64)
                        Cs = Cb[:, :, 7:519].rearrange("p h (c l) -> p h c l", l=64)[:, :, :, 0:1]
                        Ce = Cb[:, :, 8:520].rearrange("p h (c l) -> p h c l", l=64)[:, :, :, 63:64]
                        if mode == "state":
                            Cs2 = Cb[:, :, 7:519].rearrange("p h (c l) -> p h c l", l=128)[:, :, :, 0:1]
                            Ce2 = Cb[:, :, 8:520].rearrange("p h (c l) -> p h c l", l=128)[:, :, :, 127:128]
                        Lc = Lb.rearrange("p h (c l) -> p h c l", l=64)
                        if mode == "full":
                            w.tt(Lc, Cc, Cs.to_broadcast([128, GS, 8, 64]), ALU.subtract)
                        dg = dec[:, hg * GS:hg * GS + GS, :]
                        if mode == "state":
                            dg2 = dec[:, hg * GS:hg * GS + GS, 0:4]
                            L2 = Lb.rearrange("p h (c l) -> p h c l", l=128)
                            w.tt(L2, Ce2.to_broadcast([128, GS, 4, 128]),
                                 Cb[:, :, 8:520].rearrange("p h (c l) -> p h c l", l=128), ALU.subtract)
                            w.tt(dg2.unsqueeze(3), Ce2, Cs2, ALU.subtract)
                            w.act(dg2, dg2, AF.Exp)
                            w.act(Lb, Lb, AF.Exp)
                            for i, h in enumerate(hs):
                                w.tt(khat[:, i, :], Lb[:, i, :], Fb[:, i, :], ALU.mult)
                        else:
                            w.tt(dg.unsqueeze(3), Ce, Cs, ALU.subtract)
                            w.act(dg, dg, AF.Exp)
                            E2 = Cb[:, :, 8:520]
                            w.act(E2, Lb, AF.Exp, scale=-1.0)
                            w.act(Lb, Lb, AF.Exp)
                            for i, h in enumerate(hs):
                                t = ftmp()
                                w.tt(t, E2[:, i, :], Fb[:, i, :], ALU.mult)
                                w.copy("act", kt[:, h, :], t)
                                w.tt(khat[:, i, :].rearrange("p (c l) -> p c l", l=64), t.rearrange("p (c l) -> p c l", l=64),
                                     dec[:, h, :].unsqueeze(2).to_broadcast([128, 8, 64]), ALU.mult)
                def seg_C(hg):
                        hs = list(range(hg * GS, hg * GS + GS))
                        for i, h in enumerate(hs):
                            if mode == "full":
                                sq_ = wload(hg_in[:, h * 128:(h + 1) * 128], 8)
                                pq = P.psum()
                                for kc in range(8):
                                    w.mm(pq, sq_[:, kc, :], hn[:, kc, c0:c0 + 512], start=(kc == 0), stop=(kc == 7))
                                w.tt(qt[:, h, :], pq, Lb[:, i, :], ALU.mult)
                        for i, h in enumerate(hs):
                            ptb = P.psum(BF16)
                            ptb4 = ptb[:, 0:512].rearrange("p (a b) -> p a b", a=4)
                            for tl in range(4):
                                w.tr(ptb4[:, tl, :], khat[:, i, tl * 128:(tl + 1) * 128], identb)
                            w.copy("act", kTt[:, :, h * 128:(h + 1) * 128], ptb4)
                seg_A(0); seg_F(0); seg_B(0); seg_A(1); seg_F(1); seg_C(0); seg_B(1); seg_C(1)
                if sbk == 0 and mode == "full":
                    dump("qt_%d" % m, qt, BF16); dump("kt_%d" % m, kt, BF16)
                    dump("kTt_%d" % m, kTt, BF16); dump("itok_%d" % m, itok, BF16); dump("dec_%d" % m, dec)
                P.stage = "%d:L1scan" % m
                for pr in range(4):
                    if mode == "state":
                        pds = [P.psum(), P.psum()]
                        for h in range(8):
                            pv = pds[h // 4].rearrange("p (a b) -> p a b", a=4)
                            w.mm(pv[:, h % 4, :], kTt[:, pr, h * 128:(h + 1) * 128],
                                 itok[:, pr, h * 128:(h + 1) * 128], skip=True)
                        w.tt(S, S, dec[:, :, pr:pr + 1].to_broadcast([128, 8, 128]), ALU.mult)
                        for b_ in range(2):
                            pv = pds[b_].rearrange("p (a b) -> p a b", a=4)
                            w.tt(S[:, b_ * 4:(b_ + 1) * 4, :], S[:, b_ * 4:(b_ + 1) * 4, :], pv, ALU.add)
                        continue
                    tc = slice(pr * 128, (pr + 1) * 128)
                    psc = [P.psum(), P.psum()]
                    for h in range(8):
                        pv = psc[h // 4].rearrange("p (a b) -> p a b", a=4)
                        w.mm(pv[:, h % 4, :], kt[:, h, tc], qt[:, h, tc], skip=True)
                    for b_ in range(2):
                        pv = psc[b_].rearrange("p (a b) -> p a b", a=4)
                        w.tt(scb[:, b_ * 4:(b_ + 1) * 4, :], pv, maskb.unsqueeze(1).to_broadcast([128, 4, 128]), ALU.mult)
                    pso = [P.psum(), P.psum()]
                    for half in range(2):
                        hc = slice(half * 64, (half + 1) * 64)
                        pds = [P.psum(), P.psum()]
                        for h in range(8):
                            pv = pds[h // 4].rearrange("p (a b) -> p a b", a=4)
                            w.mm(pv[:, h % 4, :], kTt[hc, pr, h * 128:(h + 1) * 128], itok[hc, pr, h * 128:(h + 1) * 128],
                                 skip=True)
                        if half == 0:
                            for h in range(8):
                                pv = pso[h // 4].rearrange("p (a b) -> p a b", a=4)
                                w.mm(pv[:, h % 4, :], itok[:, pr, h * 128:(h + 1) * 128], scb[:, h, :],
                                     start=(h % 4 == 0), stop=False, skip=True)
                        for h in range(8):
                            pv = pso[h // 4].rearrange("p (a b) -> p a b", a=4)
                            w.mm(pv[:, h % 4, hc], Sbf[:, h, :], qt[:, h, pr * 128 + half * 64:pr * 128 + (half + 1) * 64],
                                 start=False, stop=(half == 1), skip=True)
                        ci = pr * 2 + half
                        w.tt(S, S, dec[:, :, ci:ci + 1].to_broadcast([128, 8, 128]), ALU.mult)
                        for b_ in range(2):
                            pv = pds[b_].rearrange("p (a b) -> p a b", a=4)
                            w.tt(S[:, b_ * 4:(b_ + 1) * 4, :], S[:, b_ * 4:(b_ + 1) * 4, :], pv, ALU.add)
                        w.copy("act", Sbf, S)
                    for b_ in range(2):
                        pv = pso[b_].rearrange("p (a b) -> p a b", a=4)
                        w.copy("act", o32[:, b_ * 4:(b_ + 1) * 4, pr * 128:(pr + 1) * 128], pv)
                if mode == "state":
                    continue
                P.stage = "%d:L1gate" % m
                SQ = kt
                w.act(SQ, o32, AF.Square)
                for h in range(8):
                    pn = P.psum()
                    w.mm(pn, onesb, SQ[:, h, :])
                    w.act(FL[:, h, :], pn, AF.Ln, bias=EPS, scale=1.0 / 128)
                w.act(FL, FL, AF.Exp, scale=-0.5)
                w.tt(FL, FL, o32, ALU.mult)
                for h in range(8):
                    w.stt(og[:, h, cs], FL[:, h, :], GNG[:, h:h + 1], SG[:, h, :], ALU.mult, ALU.mult)
            if mode == "state":
                if m == last_state:
                    w.ts(S, S, sflag[:, 0:1], None, ALU.mult)
                    w.copy("act", Sbf, S)
                continue
            dump("og_%d" % m, og, BF16)
            P.stage = "%d:L1out" % m
            for mo in range(8):
                so = wload(hg_out[:, mo * 128:(mo + 1) * 128], 8)
                for sbk in range(2):
                    cs = slice(sbk * 512, (sbk + 1) * 512)
                    pt = P.psum()
                    for kc in range(8):
                        w.mm(pt, so[:, kc, :], og[:, kc, cs], start=(kc == 0), stop=(kc == 7))
                    w.tt(hT[:, mo, cs], hT[:, mo, cs], pt, ALU.add)
            dump("hT3_%d" % m, hT)
            if stop == "mix1":
                continue
            P.stage = "%d:ffn1" % m
            ffn(1)
            dump("hT4_%d" % m, hT)

            P.stage = "%d:final" % m
            salloc_reset()
            yT = P.buf([8, 512], F32)
            yst = [P.buf([D], F32) for _ in range(2)]
            for sbk in range(2):
                cs = slice(sbk * 512, (sbk + 1) * 512)
                rmsnorm(hT[:, :, cs], yT, 512, G_FIN)
                for tl in range(4):
                    ys = yst[tl % 2]
                    for half in range(2):
                        pt = P.psum()
                        pt4 = pt.rearrange("p (a b) -> p a b", a=4)
                        for c in range(4):
                            w.tr(pt4[:, c, :], yT[:, half * 4 + c, tl * 128:(tl + 1) * 128], ident)
                        w.copy("act" if half == 0 else "dve", ys[:, half * 512:(half + 1) * 512], pt)
                    r0 = orow * T + sbk * 512 + tl * 128
                    finals.append(w.dma("sp", y[r0:r0 + 128, :], ys))
        with nc.allow_low_precision("bf16 matmul operands, fp32 accumulate"):
            P.emit(final_dmas=finals)
    nc._inames = P.inames
    return nc, dumps


def _pack_params(inp):
    pk = lambda v, n: np.ascontiguousarray(np.asarray(v, np.float32).reshape(n, 128).T)
    par = np.zeros((128, NPAR), np.float32)
    par[:, 0:16] = pk(inp["norm_mix_g"], 16)
    par[:, 16:32] = pk(inp["norm_ffn_g"], 16)
    par[:, 32:40] = pk(inp["final_g"], 8)
    par[:, 40:44] = pk(inp["cp_dw_b"][0], 4)
    par[:, 44:48] = pk(inp["cp_ln_g"][0], 4)
    par[:, 48:52] = pk(inp["cp_ln_b"][0], 4)
    par[:, 52:56] = pk(inp["cp_pool_scale"][0], 4)
    par[:, 56:64] = pk(inp["hg_gn_g"][0], 8)
    par[:, 64:72] = pk(inp["hg_lb_logits"][0], 8)
    par[:, 72:80] = pk(inp["hg_lb_logits"][1], 8)
    dw = np.asarray(inp["cp_dw_w"][0], np.float32)
    par[:, 80:204] = np.transpose(dw.reshape(31, 4, 128), (2, 1, 0)).reshape(128, 124)
    return par


def _invcnt(first_tok_list):
    n = len(first_tok_list)
    inv = np.zeros((128, n, 4, 16), np.float32)
    for m, t0 in enumerate(first_tok_list):
        for g in range(4):
            wd = 2 ** (g + 1)
            tpos = t0 + np.arange(16)
            inv[:, m, g, :] = 1.0 / np.minimum(tpos + 1, wd)
    return inv.reshape(128, -1)


_NC_CACHE = {}
PASSES8 = [("state", None), ("state", None), ("full", 0), ("full", 1)]


def _segments(xseq, t0):
    seg = np.zeros((XPAD + T, D), np.float32)
    lo = t0 - XPAD
    if lo < 0:
        seg[-lo:] = xseq[0:t0 + T]
    else:
        seg[:] = xseq[lo:t0 + T]
    return seg


def make_in_maps(inp, plan):
    par = _pack_params(inp)
    x = np.asarray(inp["x"], np.float32)
    common = {
        "par": par,
        "cp_w_in": np.ascontiguousarray(inp["cp_w_in"][0]), "cp_pool_w": np.ascontiguousarray(inp["cp_pool_w"][0]),
        "cp_w_out": np.ascontiguousarray(inp["cp_w_out"][0]), "hg_w_in": np.ascontiguousarray(inp["hg_w_in"][0]),
        "hg_w_out": np.ascontiguousarray(inp["hg_w_out"][0]),
        "ffn_w1": np.ascontiguousarray(inp["ffn_w1"]), "ffn_w3": np.ascontiguousarray(inp["ffn_w3"]),
        "ffn_w2": np.ascontiguousarray(inp["ffn_w2"]),
    }
    maps = []
    for b, t0s, flag in plan:
        d = dict(common)
        d["xp"] = np.concatenate([_segments(x[b], t0) for t0 in t0s], axis=0)
        d["invcnt"] = _invcnt(t0s)
        d["sflag"] = np.full((128, 1), flag, np.float32)
        maps.append(d)
    return maps


def plan8():
    plan = []
    for b in range(4):
        plan.append((b, [0, T, 0, T], 0.0))
        plan.append((b, [0, T, 2 * T, 3 * T], 1.0))
    return plan


def kernel(**inputs):
    inp = {k: np.asarray(v) for k, v in inputs.items()}
    if "nc" not in _NC_CACHE:
        _NC_CACHE["nc"] = build_program(passes=PASSES8)[0]
    nc = _NC_CACHE["nc"]
    maps = make_in_maps(inp, plan8())
    res = run_bass_kernel_spmd(nc, maps, core_ids=list(range(8)))
    out = np.stack([np.asarray(r["y"], np.float32) for r in res.results], axis=0)
    return out.reshape(4, SEQ, D)
```
